# Optimizing a Trainium2 kernel written in Bass

```python
import jax
import jax.numpy as jnp
from jax import lax
import numpy as np


D_MODEL = 1024
BATCH = 1
SEQ = 16384
DEPTH = 2

CHUNK = 64
N_MEM = 256
N_BRANCH = 5
BRANCH_WIDTH = D_MODEL // 2
N_HEADS = 4
HEAD_DIM = BRANCH_WIDTH // N_HEADS
SB_BLOCK = 128
GMLP_BLOCK = 128
GM_GROUPS = 4
CONV_WIDTH = 4
ROPE_BASE = 10000.0
EPS = 1e-6

kernel_name = "hybrid_stickbreak_mlstm_gmlp_retention_block"


def _in_sizes():
    w, h = BRANCH_WIDTH, N_HEADS
    return ([w] * 4
            + [w] * 5 + [h, h]
            + [w] * 3
            + [w] * 4
            + [w] * 2
            + [N_BRANCH * D_MODEL])


def rms_norm(x, g):
    xf = x.astype(jnp.float32)
    y = xf * lax.rsqrt(jnp.mean(xf * xf, axis=-1, keepdims=True) + EPS)
    return (y * g.astype(jnp.float32)).astype(x.dtype)


def layer_norm(x, g, b):
    xf = x.astype(jnp.float32)
    mu = jnp.mean(xf, axis=-1, keepdims=True)
    var = jnp.mean(jnp.square(xf - mu), axis=-1, keepdims=True)
    y = (xf - mu) * lax.rsqrt(var + EPS)
    return (y * g.astype(jnp.float32) + b.astype(jnp.float32)).astype(x.dtype)


def head_norm(t):
    tf = t.astype(jnp.float32)
    mu = jnp.mean(tf, axis=-1, keepdims=True)
    var = jnp.mean(jnp.square(tf - mu), axis=-1, keepdims=True)
    return (tf - mu) * lax.rsqrt(var + EPS)


def causal_conv(x, w):
    k = w.shape[0]
    return lax.conv_general_dilated(
        x, w[:, None, :].astype(x.dtype), window_strides=(1,), padding=((k - 1, 0),),
        dimension_numbers=('NWC', 'WIO', 'NWC'), feature_group_count=x.shape[-1])


def rotary(t, positions):
    half = t.shape[-1] // 2
    inv_freq = ROPE_BASE ** (-jnp.arange(half, dtype=jnp.float32) / half)
    ang = positions.astype(jnp.float32)[..., None] * inv_freq
    cos, sin = jnp.cos(ang)[:, :, None, :], jnp.sin(ang)[:, :, None, :]
    tf = t.astype(jnp.float32)
    t1, t2 = tf[..., :half], tf[..., half:]
    return jnp.concatenate([t1 * cos - t2 * sin, t1 * sin + t2 * cos], axis=-1)


def to_chunks(t):
    b, s, h = t.shape[:3]
    t = t.astype(jnp.float32).reshape(b, s // CHUNK, CHUNK, h, *t.shape[3:])
    return jnp.moveaxis(t, 3, 1)


def from_chunks(t):
    b, h, nc, l, d = t.shape
    return jnp.moveaxis(t, 1, 3).reshape(b, nc * l, h, d)


def stick_breaking(q, k, v):
    b, s, h, d = q.shape
    nb = s // SB_BLOCK
    qb = q.reshape(b, nb, SB_BLOCK, h, d).transpose(1, 0, 3, 2, 4)
    kf = k.transpose(0, 2, 1, 3).astype(jnp.float32)
    vf = v.transpose(0, 2, 1, 3).astype(jnp.float32)
    key_pos = jnp.arange(s)
    scale = d ** -0.5

    def block(args):
        q_blk, i = args
        z = jnp.einsum('bhqd,bhkd->bhqk', q_blk.astype(jnp.float32), kf) * scale
        q_pos = i * SB_BLOCK + jnp.arange(SB_BLOCK)
        valid = key_pos[None, :] < q_pos[:, None]
        log_keep = jnp.where(valid, -jax.nn.softplus(z), 0.0)
        later = lax.cumsum(log_keep, axis=3, reverse=True) - log_keep
        w = jnp.where(valid, jnp.exp(jax.nn.log_sigmoid(z) + later), 0.0)
        return jnp.einsum('bhqk,bhkd->bhqd', w, vf)

    out = lax.map(block, (qb, jnp.arange(nb)))
    return out.transpose(1, 0, 3, 2, 4).reshape(b, s, h, d).astype(q.dtype)


def mlstm(q, k, v, i_pre, f_pre):
    b_, s_, h_, d = q.shape
    qc, kc, vc = to_chunks(q), to_chunks(k) * d ** -0.5, to_chunks(v)
    ic = to_chunks(i_pre)
    bcum = jnp.cumsum(jax.nn.log_sigmoid(to_chunks(f_pre)), axis=-1)
    g = bcum + lax.cummax(ic - bcum, axis=3)
    b_last, a_last = bcum[..., -1], g[..., -1]
    w_end = jnp.exp(b_last[..., None] - bcum + ic - a_last[..., None])
    kv_loc = jnp.einsum('bhcld,bhcle->bhcde', kc * w_end[..., None], vc)
    n_loc = jnp.einsum('bhcl,bhcld->bhcd', w_end, kc)

    def step(carry, inp):
        c_st, n_st, m_st = carry
        bl, al, kv, nl = inp
        m_new = jnp.maximum(bl + m_st, al)
        d_old = jnp.exp(bl + m_st - m_new)
        d_loc = jnp.exp(al - m_new)
        c_new = d_old[..., None, None] * c_st + d_loc[..., None, None] * kv
        n_new = d_old[..., None] * n_st + d_loc[..., None] * nl
        return (c_new, n_new, m_new), (c_st, n_st, m_st)

    init = (jnp.zeros((b_, h_, d, d), jnp.float32), jnp.zeros((b_, h_, d), jnp.float32),
            jnp.zeros((b_, h_), jnp.float32))
    xs = (jnp.moveaxis(b_last, 2, 0), jnp.moveaxis(a_last, 2, 0),
          jnp.moveaxis(kv_loc, 2, 0), jnp.moveaxis(n_loc, 2, 0))
    _, (c_prev, n_prev, m_prev) = lax.scan(step, init, xs)
    c_prev = jnp.moveaxis(c_prev, 0, 2)
    n_prev = jnp.moveaxis(n_prev, 0, 2)
    m_prev = jnp.moveaxis(m_prev, 0, 2)

    m_t = jnp.maximum(bcum + m_prev[..., None], g)
    d_inter = jnp.exp(bcum + m_prev[..., None] - m_t)
    num = d_inter[..., None] * jnp.einsum('bhcld,bhcde->bhcle', qc, c_prev)
    den = d_inter * jnp.einsum('bhcld,bhcd->bhcl', qc, n_prev)
    causal = jnp.tril(jnp.ones((CHUNK, CHUNK), dtype=bool))
    log_d = bcum[..., :, None] - bcum[..., None, :] + ic[..., None, :] - m_t[..., :, None]
    d_intra = jnp.exp(jnp.where(causal, log_d, -jnp.inf))
    sc = jnp.einsum('bhctd,bhcsd->bhcts', qc, kc) * d_intra
    num = num + jnp.einsum('bhcts,bhcse->bhcte', sc, vc)
    den = den + jnp.sum(sc, axis=-1)
    h = num / jnp.maximum(jnp.abs(den), jnp.exp(-m_t))[..., None]
    return from_chunks(h)


def retention(q, k, v):
    b_, s_, h_, d = q.shape
    qc, kc, vc = to_chunks(q), to_chunks(k) * d ** -0.5, to_chunks(v)
    log_gamma = jnp.log(1.0 - 2.0 ** (-5.0 - jnp.arange(h_, dtype=jnp.float32)))
    idx = jnp.arange(CHUNK, dtype=jnp.float32)
    intra_decay = jnp.exp(log_gamma[:, None, None] * jnp.abs(idx[:, None] - idx[None, :]))
    sc = jnp.einsum('bhcnd,bhcmd->bhcnm', qc, kc) * intra_decay[:, None]
    intra = jnp.einsum('bhcnm,bhcme->bhcne', sc, vc)
    w_end = jnp.exp(log_gamma[:, None] * (CHUNK - 1 - idx))
    kv_loc = jnp.einsum('bhcld,bhcle->bhcde', kc * w_end[None, :, None, :, None], vc)
    chunk_decay = jnp.exp(log_gamma * CHUNK)[None, :, None, None]

    def step(state, kv):
        return chunk_decay * state + kv, state

    _, s_prev = lax.scan(step, jnp.zeros((b_, h_, d, d), jnp.float32), jnp.moveaxis(kv_loc, 2, 0))
    s_prev = jnp.moveaxis(s_prev, 0, 2)
    w_q = jnp.exp(log_gamma[:, None] * (idx + 1.0))
    inter = w_q[:, None, :, None] * jnp.einsum('bhcnd,bhcde->bhcne', qc, s_prev)
    return from_chunks(intra + inter)


def spatial_gating(u, v, ln_g, ln_b, w_s, b_s):
    b_, s_, w = u.shape
    v = layer_norm(v, ln_g, ln_b)
    vb = v.reshape(b_, s_ // GMLP_BLOCK, GMLP_BLOCK, GM_GROUPS, w // GM_GROUPS)
    chunk_id = jnp.arange(GMLP_BLOCK) // CHUNK
    mask = chunk_id[None, :] <= chunk_id[:, None]
    mixed = jnp.einsum('gpq,bnqgc->bnpgc', jnp.where(mask, w_s, 0.0).astype(v.dtype), vb)
    mixed = mixed + b_s.T[:, :, None].astype(v.dtype)
    return u * mixed.reshape(b_, s_, w)


def memory_attention(q, mem_k, mem_v):
    logits = jnp.einsum('bshd,bmhd->bhsm', q.astype(jnp.float32), mem_k.astype(jnp.float32))
    p = jax.nn.softmax(logits * q.shape[-1] ** -0.5, axis=-1)
    return jnp.einsum('bhsm,bmhd->bshd', p, mem_v.astype(jnp.float32)).astype(q.dtype)


def hybrid_layer(x, mem, positions, norm_pre, norm_post, w_in, b_igate, b_fgate, conv_q, conv_k,
                 gm_ln_g, gm_ln_b, gm_ws, gm_bs, mem_norm, w_mem_kv, w_up, w_out):
    b_, s_, _ = x.shape
    w = BRANCH_WIDTH
    heads = lambda t: t.reshape(t.shape[0], t.shape[1], N_HEADS, HEAD_DIM)
    flat = lambda t: t.reshape(b_, s_, w).astype(x.dtype)

    h = rms_norm(x, norm_pre)
    proj = h @ w_in
    split_points = [int(p) for p in np.cumsum(_in_sizes())[:-1]]
    (sb_q, sb_k, sb_v, sb_z,
     ml_q, ml_k, ml_v, ml_o, ml_z, ml_i, ml_f,
     gm_u, gm_v, gm_z,
     rt_q, rt_k, rt_v, rt_z,
     xa_q, xa_z, gates) = jnp.split(proj, split_points, axis=-1)

    y_sb = flat(stick_breaking(heads(sb_q), heads(sb_k), heads(sb_v)))
    q_ml = jax.nn.silu(causal_conv(ml_q, conv_q))
    k_ml = jax.nn.silu(causal_conv(ml_k, conv_k))
    h_ml = mlstm(heads(q_ml), heads(k_ml), heads(ml_v), ml_i + b_igate, ml_f + b_fgate)
    y_ml = flat(head_norm(h_ml)) * jax.nn.sigmoid(ml_o)
    y_gm = spatial_gating(jax.nn.gelu(gm_u), jax.nn.gelu(gm_v), gm_ln_g, gm_ln_b, gm_ws, gm_bs)
    y_rt = flat(head_norm(retention(rotary(heads(rt_q), positions), rotary(heads(rt_k), positions),
                                    heads(rt_v))))
    mkv = rms_norm(mem, mem_norm) @ w_mem_kv
    mem_k, mem_v = jnp.split(mkv, 2, axis=-1)
    y_xa = flat(memory_attention(heads(xa_q), heads(mem_k), heads(mem_v)))

    ys = jnp.stack([y_sb * jax.nn.silu(sb_z), y_ml * jax.nn.silu(ml_z), y_gm * jax.nn.silu(gm_z),
                    y_rt * jax.nn.silu(rt_z), y_xa * jax.nn.silu(xa_z)], axis=2)
    up = jnp.einsum('bsnw,nwd->bsnd', ys, w_up)
    merged = jnp.sum(jax.nn.sigmoid(gates.reshape(b_, s_, N_BRANCH, D_MODEL)) * up, axis=2)
    out = merged @ w_out
    return x + rms_norm(out, norm_post)


def setup_inputs(seed: int = 0) -> dict:
    key = jax.random.key(seed)
    ks = jax.random.split(key, 20)
    f32 = jnp.float32
    nrm = lambda k, shape, scale: jax.random.normal(k, shape, f32) * scale
    w = BRANCH_WIDTH
    d_in = sum(_in_sizes())
    x = nrm(ks[0], (BATCH, SEQ, D_MODEL), 1.0)
    mem = nrm(ks[1], (BATCH, N_MEM, D_MODEL), 1.0)
    offset = jax.random.randint(ks[2], (BATCH, 1), 0, 4096, dtype=jnp.int32)
    positions = offset + jnp.arange(SEQ, dtype=jnp.int32)[None, :]
    norm_pre = 1.0 + nrm(ks[3], (DEPTH, D_MODEL), 0.02)
    norm_post = 1.0 + nrm(ks[4], (DEPTH, D_MODEL), 0.02)
    w_in = nrm(ks[5], (DEPTH, D_MODEL, d_in), D_MODEL ** -0.5)
    b_igate = nrm(ks[6], (DEPTH, N_HEADS), 0.1)
    b_fgate = jnp.linspace(3.0, 6.0, N_HEADS, dtype=f32)[None, :] + nrm(ks[7], (DEPTH, N_HEADS), 0.1)
    conv_q = nrm(ks[8], (DEPTH, CONV_WIDTH, w), CONV_WIDTH ** -0.5)
    conv_k = nrm(ks[9], (DEPTH, CONV_WIDTH, w), CONV_WIDTH ** -0.5)
    gm_ln_g = 1.0 + nrm(ks[10], (DEPTH, w), 0.02)
    gm_ln_b = nrm(ks[11], (DEPTH, w), 0.02)
    gm_ws = nrm(ks[12], (DEPTH, GM_GROUPS, GMLP_BLOCK, GMLP_BLOCK), GMLP_BLOCK ** -0.5)
    gm_bs = 1.0 + nrm(ks[13], (DEPTH, GM_GROUPS, GMLP_BLOCK), 0.02)
    mem_norm = 1.0 + nrm(ks[14], (DEPTH, D_MODEL), 0.02)
    w_mem_kv = nrm(ks[15], (DEPTH, D_MODEL, 2 * w), D_MODEL ** -0.5)
    w_up = nrm(ks[16], (DEPTH, N_BRANCH, w, D_MODEL), w ** -0.5)
    w_out = nrm(ks[17], (DEPTH, D_MODEL, D_MODEL), D_MODEL ** -0.5)
    return {"x": x, "mem": mem, "positions": positions, "norm_pre": norm_pre, "norm_post": norm_post,
            "w_in": w_in, "b_igate": b_igate, "b_fgate": b_fgate, "conv_q": conv_q, "conv_k": conv_k,
            "gm_ln_g": gm_ln_g, "gm_ln_b": gm_ln_b, "gm_ws": gm_ws, "gm_bs": gm_bs,
            "mem_norm": mem_norm, "w_mem_kv": w_mem_kv, "w_up": w_up, "w_out": w_out}


def reference(x, mem, positions, norm_pre, norm_post, w_in, b_igate, b_fgate, conv_q, conv_k,
              gm_ln_g, gm_ln_b, gm_ws, gm_bs, mem_norm, w_mem_kv, w_up, w_out):
    for l in range(DEPTH):
        x = hybrid_layer(x, mem, positions, norm_pre[l], norm_post[l], w_in[l], b_igate[l], b_fgate[l],
                         conv_q[l], conv_k[l], gm_ln_g[l], gm_ln_b[l], gm_ws[l], gm_bs[l],
                         mem_norm[l], w_mem_kv[l], w_up[l], w_out[l])
    return x
```

```python
import numpy as np
import concourse.bass as bass
import concourse.mybir as mybir
from concourse.bass_utils import run_bass_kernel_spmd

F32 = mybir.dt.float32
BF16 = mybir.dt.bfloat16
I32 = mybir.dt.int32
AF = mybir.ActivationFunctionType
ALU = mybir.AluOpType
AX = mybir.AxisListType

D = 1024
W = 512
NH = 4
HD = 128
DEPTH = 2
NMEM = 256
DIN = 14344
EPS = 1e-6
TG = 512
O_SBQ, O_SBK, O_SBV, O_SBZ = 0, 512, 1024, 1536
O_MLQ, O_MLK, O_MLV, O_MLO, O_MLZ, O_MLI, O_MLF = 2048, 2560, 3072, 3584, 4096, 4608, 4612
O_GMU, O_GMV, O_GMZ = 4616, 5128, 5640
O_RTQ, O_RTK, O_RTV, O_RTZ = 6152, 6664, 7176, 7688
O_XAQ, O_XAZ = 8200, 8712
O_GATES = 9224
NEG = -30000.0
SCALE = HD ** -0.5


class T:
    __slots__ = ("t", "w", "r", "name")

    def __init__(self, t, name=""):
        self.t = t
        self.w = None
        self.r = {}
        self.name = name

    def __getitem__(self, idx):
        return self.t[idx]


class Sched:
    EPOCH = 30000

    def __init__(self, nc, ndma_sems=24):
        self.nc = nc
        self.eng = {"pe": nc.tensor, "act": nc.scalar, "dve": nc.vector, "pool": nc.gpsimd, "sp": nc.sync}
        self.sem = {}
        self.cnt = {}
        self.seen = {e: {} for e in self.eng}
        self.nsem = 0
        for e in ("pe", "act", "dve", "pool"):
            self.sem[e] = self._newsem(e)
            self.cnt[e] = 0
        self.dma_sems = [self._newsem("dma%d" % i) for i in range(ndma_sems)]
        self.dma_cnt = [0] * ndma_sems
        self.dma_rr = 0
        self.ninst = 0

    def _newsem(self, name):
        self.nsem += 1
        return self.nc.semaphore("s_%s_%d" % (name, self.nsem)).__enter__()

    def _need(self, e, ev):
        sem, val = ev
        key = id(sem)
        if self.seen[e].get(key, 0) < val:
            self.eng[e].wait_ge(sem, val)
            self.seen[e][key] = val

    def _deps(self, e, reads, writes):
        for t in reads:
            if t.w is not None and not (e == "pe" and t.w[2] == "pe"):
                self._need(e, t.w[:2])
        for t in writes:
            if t.w is not None and not (e == "pe" and t.w[2] == "pe"):
                self._need(e, t.w[:2])
            for (re_, ev) in t.r.items():
                if not (e == "pe" and re_ == "pe"):
                    self._need(e, ev)

    def _commit(self, e, ev, reads, writes):
        for t in reads:
            t.r[e] = ev
        for t in writes:
            t.w = (ev[0], ev[1], e)
            t.r = {}

    def op(self, e, fn, reads=(), writes=()):
        self._deps(e, reads, writes)
        if self.cnt[e] >= self.EPOCH:
            self.sem[e] = self._newsem(e)
            self.cnt[e] = 0
        ins = fn(self.eng[e])
        self.cnt[e] += 1
        ins.then_inc(self.sem[e], 1)
        ev = (self.sem[e], self.cnt[e])
        self._commit(e, ev, reads, writes)
        self.ninst += 1
        return ins

    def dma(self, out, in_, reads=(), writes=(), q="sp", **kw):
        k = self.dma_rr
        self.dma_rr = (k + 1) % len(self.dma_sems)
        sem = self.dma_sems[k]
        if self.dma_cnt[k] > 0:
            self._need(q, (sem, self.dma_cnt[k]))
        self._deps(q, reads, writes)
        self.dma_cnt[k] += 16
        self.eng[q].dma_start(out=out, in_=in_, **kw).then_inc(sem, 16)
        ev = (sem, self.dma_cnt[k])
        self._commit("dma%d" % k, ev, reads, writes)
        self.ninst += 1
        return ev

    def barrier(self):
        evs = [(self.sem[e], self.cnt[e]) for e in ("pe", "act", "dve", "pool") if self.cnt[e] > 0]
        evs += [(self.dma_sems[k], self.dma_cnt[k]) for k in range(len(self.dma_sems)) if self.dma_cnt[k] > 0]
        for e in self.eng:
            for ev in evs:
                if not (e in self.sem and ev[0] is self.sem[e]):
                    self._need(e, ev)

    def finish(self):
        for k in range(len(self.dma_sems)):
            if self.dma_cnt[k] > 0:
                self._need("sp", (self.dma_sems[k], self.dma_cnt[k]))


def host_consts():
    c = {}
    i = np.arange(128)
    c["ident"] = np.eye(128, dtype=np.float32)
    c["negU"] = -(i[:, None] >= i[None, :]).astype(np.float32)
    c["negL"] = -(i[:, None] < i[None, :]).astype(np.float32)
    q = np.arange(512)
    am = np.zeros((128, 4, 512), np.float32)
    for a in range(4):
        am[:, a, :] = np.where(i[:, None] + 128 * a >= q[None, :], NEG, 0.0)
    c["amask"] = am.reshape(128, 2048)
    same = (i[:, None] // 64) == (i[None, :] // 64)
    c["mlmask"] = np.where(same & (i[:, None] <= i[None, :]), 0.0, NEG).astype(np.float32)
    lg = np.log(np.float32(1.0) - np.float32(2.0) ** (-5.0 - np.arange(4, dtype=np.float32))).astype(np.float32)
    dec = np.zeros((128, 4, 128), np.float32)
    for h in range(4):
        dec[:, h, :] = np.where(same, np.exp(lg[h] * np.abs(i[:, None] - i[None, :]).astype(np.float32)), 0.0)
    c["rdecay"] = dec.reshape(128, 512)
    wq = np.zeros((128, 4, 64), np.float32)
    for h in range(4):
        wq[:, h, :] = np.exp(lg[h] * (np.arange(64).astype(np.float32) + 1.0))[None, :]
    c["rwq"] = wq.reshape(128, 256)
    we = np.zeros((128, 4), np.float32)
    for h in range(4):
        we[:, h] = np.exp(lg[h] * (63.0 - (i % 64).astype(np.float32)))
    c["rwend"] = we
    c["rcd"] = np.exp(lg * np.float32(64.0)).astype(np.float32)
    sel = np.zeros((128, 4, 128), np.float32)
    for h in range(4):
        sel[h, h, :] = 1.0
    c["sel"] = sel.reshape(128, 512)
    i4 = np.zeros((128, 4), np.float32)
    i4[:4, :4] = np.eye(4)
    c["i4"] = i4
    rm = np.ones((128, 512), np.float32)
    rm[:, ::64] = 0.0
    c["resetm"] = rm
    nr = np.zeros((128, 512), np.float32)
    nr[:, ::64] = -1e30
    c["negreset"] = nr
    half = 64
    invf = (np.float32(10000.0) ** (-np.arange(half, dtype=np.float32) / np.float32(half))).astype(np.float32)
    c["invf"] = np.concatenate([invf, invf])[:, None].astype(np.float32)
    c["sgn"] = np.concatenate([-np.ones(64), np.ones(64)])[:, None].astype(np.float32)
    gmm = np.ones((128, 128), np.float32)
    gmm[64:, :64] = 0.0
    c["gmmask"] = gmm
    names = ["ident", "negU", "negL", "mlmask", "rdecay", "rwq", "rwend", "sel", "i4", "resetm",
             "negreset", "invf", "sgn", "gmmask"]
    offs = {}
    o = 0
    for n in names:
        offs[n] = (o, c[n].shape[1])
        o += c[n].shape[1]
    arr = np.concatenate([c[n] for n in names], axis=1).astype(np.float32)
    return arr, offs, c["rcd"], c["amask"]


def build(TT):
    NG = TT // TG
    NTILE = TT // 128
    carr, coffs, rcd, amask_np = host_consts()
    NCF = carr.shape[1]
    nc = bass.Bass("TRN2", target_bir_lowering=False)
    S = Sched(nc)
    sbytes = [0]

    def dram(name, shape, dt, kind):
        return nc.dram_tensor(name, shape, dt, kind=kind).ap()

    x_in = dram("x", [TT, D], F32, "ExternalInput")
    mem_in = dram("mem", [NMEM, D], F32, "ExternalInput")
    pos_in = dram("positions", [1, TT], I32, "ExternalInput")
    norm_pre = dram("norm_pre", [DEPTH, D], F32, "ExternalInput")
    norm_post = dram("norm_post", [DEPTH, D], F32, "ExternalInput")
    w_in = dram("w_in", [DEPTH, D, DIN], F32, "ExternalInput")
    b_ig = dram("b_igate", [DEPTH, NH], F32, "ExternalInput")
    b_fg = dram("b_fgate", [DEPTH, NH], F32, "ExternalInput")
    conv_q = dram("conv_q", [DEPTH, 4, W], F32, "ExternalInput")
    conv_k = dram("conv_k", [DEPTH, 4, W], F32, "ExternalInput")
    gm_ln_g = dram("gm_ln_g", [DEPTH, W], F32, "ExternalInput")
    gm_ln_b = dram("gm_ln_b", [DEPTH, W], F32, "ExternalInput")
    gm_ws = dram("gm_ws", [DEPTH, 4, 128, 128], F32, "ExternalInput")
    gm_bs = dram("gm_bs", [DEPTH, 4, 128], F32, "ExternalInput")
    mem_norm = dram("mem_norm", [DEPTH, D], F32, "ExternalInput")
    w_mem_kv = dram("w_mem_kv", [DEPTH, D, 2 * W], F32, "ExternalInput")
    w_up = dram("w_up", [DEPTH, 5, W, D], F32, "ExternalInput")
    w_out = dram("w_out", [DEPTH, D, D], F32, "ExternalInput")
    cst = dram("cst", [128, NCF], F32, "ExternalInput")
    cst_am = dram("cst_am", [128, 2048], F32, "ExternalInput")
    y_out = dram("y", [TT, D], F32, "ExternalOutput")
    wb = dram("wb", [DEPTH, D, DIN], BF16, "Internal")
    wsw = dram("wsw", [DEPTH, D, 1024], BF16, "Internal")
    wkvb = dram("wkvb", [DEPTH, D, 1024], BF16, "Internal")
    wupb = dram("wupb", [DEPTH, 5 * W, D], BF16, "Internal")
    woutb = dram("woutb", [DEPTH, D, D], BF16, "Internal")
    x1 = dram("x1", [TT, D], F32, "Internal")
    kTs = dram("kTs", [NH, 128, TT], BF16, "Internal")
    vS = dram("vS", [NH, 128, NTILE, 128], BF16, "Internal")
    d_wb, d_wsw, d_wkvb, d_wupb, d_woutb = T(wb), T(wsw), T(wkvb), T(wupb), T(woutb)
    d_x1, d_kTs, d_vS, d_y = T(x1), T(kTs), T(vS), T(y_out)

    import contextlib
    scopes = [contextlib.ExitStack()]
    uniq = [0]
    cur_bytes = [0]
    max_bytes = [0]

    def sb(name, shape, dt=F32):
        n = 1
        for s in shape[1:]:
            n *= s
        nb = n * (4 if dt in (F32, I32) else 2)
        cur_bytes[0] += nb
        max_bytes[0] = max(max_bytes[0], cur_bytes[0])
        uniq[0] += 1
        t = scopes[-1].enter_context(nc.sbuf_tensor("%s_%d" % (name, uniq[0]), list(shape), dt))
        scopes[-1].callback(lambda: cur_bytes.__setitem__(0, cur_bytes[0] - nb))
        return T(t, name)

    @contextlib.contextmanager
    def branch():
        scopes.append(contextlib.ExitStack())
        try:
            yield
        finally:
            S.barrier()
            rot.clear()
            scopes.pop().close()

    banks = [T(nc.psum_tensor("bank%d" % i, [128, 512], F32).__enter__(), "bank%d" % i) for i in range(8)]
    bank_rr = [0]
    gen_banks = [6, 7]

    def nbank():
        b = banks[gen_banks[bank_rr[0] % len(gen_banks)]]
        bank_rr[0] += 1
        return b

    rot = {}

    def rtile(key, n, mk):
        if key not in rot:
            rot[key] = [[mk(i) for i in range(n)], 0]
        lst, i = rot[key]
        rot[key][1] = i + 1
        return lst[i % n]

    cf = sb("cf", [128, NCF])
    S.dma(cf[:], cst[:, :], writes=[cf])

    def cview(n):
        o, w_ = coffs[n]
        return cf[:, o:o + w_]

    def cbf(name, n, width):
        t = sb(name, [128, width], BF16)
        S.op("dve", lambda e: e.tensor_copy(out=t[:], in_=cview(n)), reads=[cf], writes=[t])
        return t

    identb = cbf("identb", "ident", 128)
    negUb = cbf("negUb", "negU", 128)
    negLb = cbf("negLb", "negL", 128)
    mlmaskb = cbf("mlmaskb", "mlmask", 128)
    amaskb = sb("amaskb", [128, 2048], BF16)
    onesb = sb("onesb", [128, 128], BF16)
    S.op("dve", lambda e: e.memset(onesb[:], 1.0), writes=[onesb])
    mhalf = sb("mhalf", [128, 1])
    S.op("dve", lambda e: e.memset(mhalf[:], -0.5), writes=[mhalf])
    identf = cview("ident")

    def selv(h):
        o, _ = coffs["sel"]
        return cf[0:4, o + h * 128:o + (h + 1) * 128]

    i4v = cf[0:4, coffs["i4"][0]:coffs["i4"][0] + 4]

    cast_rr = [0]

    def cast_copy(out_t, out_ap, in_t, in_ap):
        k = cast_rr[0] % 3
        cast_rr[0] += 1
        if k == 0:
            S.op("dve", lambda e: e.tensor_copy(out=out_ap, in_=in_ap), reads=[in_t], writes=[out_t])
        elif k == 1:
            S.op("act", lambda e: e.copy(out=out_ap, in_=in_ap), reads=[in_t], writes=[out_t])
        else:
            S.op("pool", lambda e: e.tensor_copy(out=out_ap, in_=in_ap), reads=[in_t], writes=[out_t])

    PW = 1024

    def prep_matrix(src2d, dst_t, dst2d, nrows, ncols):
        for r0 in range(0, nrows, 128):
            for c0 in range(0, ncols, PW):
                cw = min(PW, ncols - c0)
                f = rtile("prep_f", 3, lambda i: sb("prep_f%d" % i, [128, PW], F32))
                b = rtile("prep_b", 3, lambda i: sb("prep_b%d" % i, [128, PW], BF16))
                S.dma(f[:, 0:cw], src2d[r0:r0 + 128, c0:c0 + cw], writes=[f])
                cast_copy(b, b[:, 0:cw], f, f[:, 0:cw])
                S.dma(dst2d[r0:r0 + 128, c0:c0 + cw], b[:, 0:cw], reads=[b], writes=[dst_t])

    with branch():
        for hf in range(2):
            f = rtile("prep_f", 3, lambda i: sb("prep_f%d" % i, [128, PW], F32))
            S.dma(f[:, :], cst_am[:, hf * 1024:(hf + 1) * 1024], writes=[f])
            S.op("dve", lambda e: e.tensor_copy(out=amaskb[:, hf * 1024:(hf + 1) * 1024], in_=f[:, :]),
                 reads=[f], writes=[amaskb])
        for l in range(DEPTH):
            prep_matrix(w_in[l], d_wb, wb[l], D, DIN)
            prep_matrix(w_mem_kv[l], d_wkvb, wkvb[l], D, 1024)
            prep_matrix(w_up[l].rearrange("n w d -> (n w) d"), d_wupb, wupb[l], 5 * W, D)
            prep_matrix(w_out[l], d_woutb, woutb[l], D, D)
            for r0 in range(0, D, 128):
                f = rtile("prep_f", 3, None)
                b = rtile("prep_b", 3, None)
                S.dma(f[:, 0:1024], w_in[l][r0:r0 + 128, O_RTQ:O_RTQ + 1024], writes=[f])
                cast_copy(b, b[:, 0:1024], f, f[:, 0:1024])
                bv = b[:, 0:1024].rearrange("p (h two d) -> p h two d", two=2, d=64)
                dv = wsw[l][r0:r0 + 128, :].rearrange("p (h two d) -> p h two d", two=2, d=64)
                for s_ in range(2):
                    S.dma(dv[:, :, 1 - s_, :], bv[:, :, s_, :], reads=[b], writes=[d_wsw])

    hT = sb("hT", [128, 8, TG], BF16)
    ss4 = sb("ss4", [128, 4])
    rstd4 = sb("rstd4", [128, 4])
    gcol_pre = sb("gcol_pre", [128, 8])
    gcol_mem = sb("gcol_mem", [128, 8])
    gpost = sb("gpost", [128, D])
    lng = sb("lng", [128, W])
    lnb = sb("lnb", [128, W])
    cq = sb("cq", [128, NH, 4])
    ck = sb("ck", [128, NH, 4])
    gbs = sb("gbs", [128, 4])
    gwT = sb("gwT", [128, 4, 128], BF16)
    big = sb("big", [4, 1])
    bfg = sb("bfg", [4, 1])
    nbfg = sb("nbfg", [4, 1])
    memkT = sb("memkT", [128, NH, NMEM], BF16)
    memv = sb("memv", [128, 2, W], BF16)
    Wt = [sb("Wt%d" % i, [128, 8, 512], BF16) for i in range(2)]
    wt_rr = [0]
    ysT = [sb("ysT%d" % n, [128, NH, TG], BF16) for n in range(5)]
    Cst = sb("Cst", [128, NH, 129])
    Rst = sb("Rst", [128, NH, 128])
    mcar = sb("mcar", [4, 1])
    halo = sb("halo", [128, 2, NH, 3])
    st1 = sb("st1", [128, 4])
    st2 = sb("st2", [128, 4])
    st3 = sb("st3", [128, 4])
    den4 = sb("den4", [128, 4])

    def loadW(src_l, c0, ncols, src_t=None):
        t = Wt[wt_rr[0] % len(Wt)]
        wt_rr[0] += 1
        S.dma(t[:, :, 0:ncols], src_l.rearrange("(c p) n -> p c n", p=128)[:, :, c0:c0 + ncols],
              reads=[src_t if src_t is not None else d_wb], writes=[t])
        return t

    def rsqrt_cols(out_t, in_t, n, scale):
        S.op("dve", lambda e: e.tensor_scalar(out=in_t[:, 0:n], in0=in_t[:, 0:n], scalar1=scale, scalar2=EPS,
                                              op0=ALU.mult, op1=ALU.add), reads=[in_t], writes=[in_t])
        S.op("pool", lambda e: e.tensor_tensor(out=out_t[:, 0:n], in0=in_t[:, 0:n],
                                               in1=mhalf[:, 0:1].broadcast_to([128, n]), op=ALU.pow),
             reads=[in_t, mhalf], writes=[out_t])

    def norm_transpose(src_tiles, gcol, dstT, ntile, junk, xn):
        for i in range(ntile):
            S.op("act", lambda e: e.activation(out=junk[:], in_=src_tiles[i][:], func=AF.Square,
                                               accum_out=ss4[:, i:i + 1]),
                 reads=[src_tiles[i]], writes=[junk, ss4])
        rsqrt_cols(rstd4, ss4, ntile, 1.0 / D)
        for i in range(ntile):
            xb = xn[i % 2]
            S.op("dve", lambda e: e.tensor_scalar(out=xb[:], in0=src_tiles[i][:], scalar1=rstd4[:, i:i + 1],
                                                  scalar2=None, op0=ALU.mult),
                 reads=[src_tiles[i], rstd4], writes=[xb])
            ps = nbank()
            psb = ps[:].bitcast(BF16)
            for kc in range(8):
                S.op("pe", lambda e: e.transpose(out=psb[:, kc * 128:(kc + 1) * 128],
                                                 in_=xb[:, kc * 128:(kc + 1) * 128], identity=identb[:]),
                     reads=[xb, identb], writes=[ps])
            S.op("dve", lambda e: e.tensor_tensor(
                out=dstT[:, :, i * 128:(i + 1) * 128],
                in0=psb.rearrange("p (c t) -> p c t", t=128),
                in1=gcol[:, :].unsqueeze(2).broadcast_to([128, 8, 128]), op=ALU.mult),
                 reads=[ps, gcol], writes=[dstT])

    def mm_acc(ps, out_ap, pairs, extra_reads=(), start=True, stop=True):
        n = len(pairs)
        for i, (l_ap, r_ap) in enumerate(pairs):
            S.op("pe", lambda e: e.matmul(out_ap, lhsT=l_ap, rhs=r_ap, start=(start and i == 0),
                                          stop=(stop and i == n - 1)),
                 reads=list(extra_reads), writes=[ps])

    def proj_fm(wt, j, src_T, ps, out_ap, ntok=TG, extra=()):
        mm_acc(ps, out_ap, [(wt[:, kc, j * 128:(j + 1) * 128], src_T[:, kc, 0:ntok]) for kc in range(8)],
               extra_reads=[wt, src_T] + list(extra))

    def proj_tm(wt, i, src_T, ps, out_ap, ncols=512, extra=()):
        mm_acc(ps, out_ap, [(src_T[:, kc, i * 128:(i + 1) * 128], wt[:, kc, 0:ncols]) for kc in range(8)],
               extra_reads=[wt, src_T] + list(extra))

    def gelu_tanh(dst_t, dst_ap, ps, src_ap, tmpA, tmpB):
        S.op("act", lambda e: e.copy(out=tmpA[:], in_=src_ap), reads=[ps], writes=[tmpA])
        S.op("dve", lambda e: e.tensor_tensor(out=tmpB[:], in0=tmpA[:], in1=tmpA[:], op=ALU.mult),
             reads=[tmpA], writes=[tmpB])
        S.op("dve", lambda e: e.tensor_scalar(out=tmpB[:], in0=tmpB[:], scalar1=0.044715, scalar2=1.0,
                                              op0=ALU.mult, op1=ALU.add), reads=[tmpB], writes=[tmpB])
        S.op("dve", lambda e: e.tensor_tensor(out=tmpB[:], in0=tmpB[:], in1=tmpA[:], op=ALU.mult),
             reads=[tmpB, tmpA], writes=[tmpB])
        S.op("act", lambda e: e.activation(out=tmpB[:], in_=tmpB[:], func=AF.Sigmoid, scale=1.5957691216057308),
             reads=[tmpB], writes=[tmpB])
        S.op("dve", lambda e: e.tensor_tensor(out=dst_ap, in0=tmpB[:], in1=tmpA[:], op=ALU.mult),
             reads=[tmpB, tmpA], writes=[dst_t])

    def head_norm_tok(src_t, hsq):
        S.op("dve", lambda e: e.tensor_reduce(out=st1[:], in_=src_t[:], axis=AX.X, op=ALU.add),
             reads=[src_t], writes=[st1])
        S.op("pool", lambda e: e.tensor_tensor(out=hsq[:], in0=src_t[:], in1=src_t[:], op=ALU.mult),
             reads=[src_t], writes=[hsq])
        S.op("dve", lambda e: e.tensor_reduce(out=st2[:], in_=hsq[:], axis=AX.X, op=ALU.add),
             reads=[hsq], writes=[st2])
        S.op("dve", lambda e: e.tensor_scalar(out=st1[:], in0=st1[:], scalar1=1.0 / 128, scalar2=None,
                                              op0=ALU.mult), reads=[st1], writes=[st1])
        S.op("dve", lambda e: e.tensor_tensor(out=st3[:], in0=st1[:], in1=st1[:], op=ALU.mult),
             reads=[st1], writes=[st3])
        S.op("dve", lambda e: e.scalar_tensor_tensor(out=st2[:], in0=st2[:], scalar=1.0 / 128, in1=st3[:],
                                                     op0=ALU.mult, op1=ALU.subtract),
             reads=[st2, st3], writes=[st2])
        rsqrt_cols(st3, st2, 4, 1.0)
        S.op("dve", lambda e: e.tensor_tensor(out=src_t[:], in0=src_t[:],
                                              in1=st1[:, :].unsqueeze(2).broadcast_to([128, NH, 128]),
                                              op=ALU.subtract), reads=[src_t, st1], writes=[src_t])
        S.op("dve", lambda e: e.tensor_tensor(out=src_t[:], in0=src_t[:],
                                              in1=st3[:, :].unsqueeze(2).broadcast_to([128, NH, 128]),
                                              op=ALU.mult), reads=[src_t, st3], writes=[src_t])

    def tok_to_fm(src_b, dst_T, i):
        ps = nbank()
        psb = ps[:].bitcast(BF16)
        for h in range(NH):
            S.op("pe", lambda e: e.transpose(out=psb[:, h * 128:(h + 1) * 128], in_=src_b[:, h * 128:(h + 1) * 128],
                                             identity=identb[:]), reads=[src_b, identb], writes=[ps])
        S.op("act", lambda e: e.copy(out=dst_T[:, :, i * 128:(i + 1) * 128],
                                     in_=psb[:, 0:512].rearrange("p (h t) -> p h t", t=128)),
             reads=[ps], writes=[dst_T])

    for l in range(DEPTH):
        xsrc, xsrc_t = (x_in, None) if l == 0 else (x1, d_x1)
        xdst, xdst_t = (x1, d_x1) if l < DEPTH - 1 else (y_out, d_y)
        with branch():
            S.dma(gcol_pre[:], norm_pre[l].rearrange("(c p) -> p c", p=128), writes=[gcol_pre],
                  allow_slow_non_contiguous=True)
            S.dma(gcol_mem[:], mem_norm[l].rearrange("(c p) -> p c", p=128), writes=[gcol_mem],
                  allow_slow_non_contiguous=True)
            S.dma(gpost[:], norm_post[l:l + 1, :].broadcast_to([128, D]), writes=[gpost])
            S.dma(lng[:], gm_ln_g[l:l + 1, :].broadcast_to([128, W]), writes=[lng])
            S.dma(lnb[:], gm_ln_b[l:l + 1, :].broadcast_to([128, W]), writes=[lnb])
            for h in range(NH):
                S.dma(cq[:, h, :], conv_q[l][:, h * 128:(h + 1) * 128].rearrange("i d -> d i"), writes=[cq],
                      allow_slow_non_contiguous=True)
                S.dma(ck[:, h, :], conv_k[l][:, h * 128:(h + 1) * 128].rearrange("i d -> d i"), writes=[ck],
                      allow_slow_non_contiguous=True)
            S.dma(gbs[:], gm_bs[l].rearrange("g p -> p g"), writes=[gbs], allow_slow_non_contiguous=True)
            gw_raw = sb("gw_raw", [128, 4, 128])
            S.dma(gw_raw[:], gm_ws[l].rearrange("g p q -> p g q"), writes=[gw_raw])
            S.dma(big[:], b_ig[l].rearrange("(h o) -> h o", o=1), writes=[big])
            S.dma(bfg[:], b_fg[l].rearrange("(h o) -> h o", o=1), writes=[bfg])
            S.op("dve", lambda e: e.tensor_scalar(out=nbfg[:], in0=bfg[:], scalar1=-1.0, scalar2=None, op0=ALU.mult),
                 reads=[bfg], writes=[nbfg])
            for g_ in range(4):
                ps = nbank()
                S.op("pe", lambda e: e.transpose(out=ps[:, 0:128], in_=gw_raw[:, g_, :], identity=identf),
                     reads=[gw_raw, cf], writes=[ps])
                S.op("dve", lambda e: e.tensor_tensor(out=gwT[:, g_, :], in0=ps[:, 0:128], in1=cview("gmmask"),
                                                      op=ALU.mult), reads=[ps, cf], writes=[gwT])
            mt_ = [sb("memx%d" % i, [128, D]) for i in range(2)]
            junk = sb("junk", [128, D])
            xn = [sb("xn%d" % i, [128, D], BF16) for i in range(2)]
            memT = sb("memT", [128, 8, NMEM], BF16)
            for i in range(2):
                S.dma(mt_[i][:], mem_in[i * 128:(i + 1) * 128, :], writes=[mt_[i]])
            norm_transpose(mt_, gcol_mem, memT, 2, junk, xn)
            wkv = loadW(wkvb[l], 0, 512, d_wkvb)
            for h in range(NH):
                ps = nbank()
                proj_fm(wkv, h, memT, ps, ps[:, 0:NMEM], ntok=NMEM)
                S.op("act", lambda e: e.copy(out=memkT[:, h, :], in_=ps[:, 0:NMEM]), reads=[ps], writes=[memkT])
            wkv = loadW(wkvb[l], 512, 512, d_wkvb)
            for i in range(2):
                ps = nbank()
                proj_tm(wkv, i, memT, ps, ps[:, :])
                S.op("act", lambda e: e.copy(out=memv[:, i, :], in_=ps[:, :]), reads=[ps], writes=[memv])
            S.op("dve", lambda e: e.memset(Cst[:], 0.0), writes=[Cst])
            S.op("dve", lambda e: e.memset(Rst[:], 0.0), writes=[Rst])
            S.op("dve", lambda e: e.memset(mcar[:], 0.0), writes=[mcar])
            S.op("dve", lambda e: e.memset(halo[:], 0.0), writes=[halo])

        for g in range(NG):
            t0 = g * TG
            with branch():
                xt = [sb("xt%d" % i, [128, D]) for i in range(4)]
                junk = sb("junk", [128, D])
                xn = [sb("xn%d" % i, [128, D], BF16) for i in range(2)]
                for i in range(4):
                    S.dma(xt[i][:], xsrc[t0 + i * 128:t0 + (i + 1) * 128, :],
                          reads=[xsrc_t] if xsrc_t is not None else [], writes=[xt[i]])
                norm_transpose(xt, gcol_pre, hT, 4, junk, xn)

            with branch():
                qT = sb("qT", [128, NH, TG], BF16)
                kT = sb("kT", [128, NH, TG], BF16)
                vtok = sb("vtok", [128, 4, NH, 128], BF16)
                szT = sb("szT", [128, NH, TG])
                kpc = [sb("kpc%d" % i, [128, 2048], BF16) for i in range(3)]
                vpc = [sb("vpc%d" % i, [128, 16, 128], BF16) for i in range(3)]
                e_t = [sb("e_t%d" % i, [128, TG]) for i in range(4)]
                sp_t = [sb("sp_t%d" % i, [128, TG], BF16) for i in range(6)]
                g_t = [sb("g_t%d" % i, [128, TG]) for i in range(3)]
                w_t = [sb("w_t%d" % i, [128, TG], BF16) for i in range(4)]
                wt = loadW(wb[l], O_SBQ, 512)
                for h in range(NH):
                    ps = nbank()
                    proj_fm(wt, h, hT, ps, ps[:, :])
                    S.op("act", lambda e: e.activation(out=qT[:, h, :], in_=ps[:, :], func=AF.Copy, scale=SCALE),
                         reads=[ps], writes=[qT])
                wt = loadW(wb[l], O_SBK, 512)
                for h in range(NH):
                    ps = nbank()
                    proj_fm(wt, h, hT, ps, ps[:, :])
                    S.op("dve", lambda e: e.tensor_copy(out=kT[:, h, :], in_=ps[:, :]), reads=[ps], writes=[kT])
                S.dma(kTs[:, :, t0:t0 + TG].rearrange("h p t -> p h t"), kT[:], reads=[kT], writes=[d_kTs])
                wt = loadW(wb[l], O_SBV, 512)
                for i in range(4):
                    ps = nbank()
                    proj_tm(wt, i, hT, ps, ps[:, :])
                    S.op("act", lambda e: e.copy(out=vtok[:, i, :, :],
                                                 in_=ps[:, :].rearrange("p (h d) -> p h d", d=128)),
                         reads=[ps], writes=[vtok])
                for h in range(NH):
                    S.dma(vS[h, :, g * 4:(g + 1) * 4, :], vtok[:, :, h, :], reads=[vtok], writes=[d_vS])
                wt = loadW(wb[l], O_SBZ, 512)
                for h in range(NH):
                    ps = nbank()
                    proj_fm(wt, h, hT, ps, ps[:, :])
                    S.op("act", lambda e: e.activation(out=szT[:, h, :], in_=ps[:, :], func=AF.Silu),
                         reads=[ps], writes=[szT])
                nblk = 4 * g + 4
                for hp in range(2):
                    heads = (2 * hp, 2 * hp + 1)
                    prev_sp = {}
                    pcs = {}
                    for j in range(nblk - 1, -1, -1):
                        pc = j // 16
                        for hi, h in enumerate(heads):
                            if (h, pc) not in pcs:
                                kt_ = rtile("kpc", 3, lambda i: kpc[i])
                                vt_ = rtile("vpc", 3, lambda i: vpc[i])
                                nb_ = min(16, nblk - pc * 16)
                                S.dma(kt_[:, 0:nb_ * 128], kTs[h, :, pc * 2048:pc * 2048 + nb_ * 128],
                                      reads=[d_kTs], writes=[kt_])
                                S.dma(vt_[:, 0:nb_, :], vS[h, :, pc * 16:pc * 16 + nb_, :], reads=[d_vS],
                                      writes=[vt_])
                                pcs = {k_: v_ for k_, v_ in pcs.items() if k_[0] != h}
                                pcs[(h, pc)] = (kt_, vt_)
                            kt_, vt_ = pcs[(h, pc)]
                            jj = j - pc * 16
                            first = (j == nblk - 1)
                            Z = banks[hi]
                            P = banks[2 + hi]
                            O = banks[4 + hi]
                            a = j - 4 * g
                            S.op("pe", lambda e: e.matmul(Z[:, :], lhsT=kt_[:, jj * 128:(jj + 1) * 128],
                                                          rhs=qT[:, h, :], start=True, stop=(a < 0)),
                                 reads=[kt_, qT], writes=[Z])
                            if a >= 0:
                                S.op("pe", lambda e: e.matmul(Z[:, :], lhsT=identb[:],
                                                              rhs=amaskb[:, a * 512:(a + 1) * 512],
                                                              start=False, stop=True), reads=[identb, amaskb],
                                     writes=[Z])
                            et = rtile("e_t", 4, lambda i: e_t[i])
                            spt = rtile("sp_t", 6, lambda i: sp_t[i])
                            S.op("act", lambda e: e.activation(out=et[:], in_=Z[:, :], func=AF.Exp),
                                 reads=[Z], writes=[et])
                            S.op("act", lambda e: e.activation(out=spt[:], in_=et[:], func=AF.Ln, bias=1.0),
                                 reads=[et], writes=[spt])
                            if not first:
                                psp = prev_sp[h]
                                S.op("pe", lambda e: e.matmul(P[:, :], lhsT=negLb[:], rhs=psp[:], start=False,
                                                              stop=False, skip_group_check=True),
                                     reads=[negLb, psp], writes=[P])
                            S.op("pe", lambda e: e.matmul(P[:, :], lhsT=negUb[:], rhs=spt[:], start=first, stop=True,
                                                          skip_group_check=True), reads=[negUb, spt], writes=[P])
                            prev_sp[h] = spt
                            gt = rtile("g_t", 3, lambda i: g_t[i])
                            S.op("act", lambda e: e.activation(out=gt[:], in_=P[:, :], func=AF.Exp),
                                 reads=[P], writes=[gt])
                            wtl = rtile("w_t", 4, lambda i: w_t[i])
                            S.op("dve", lambda e: e.tensor_tensor(out=wtl[:], in0=et[:], in1=gt[:], op=ALU.mult),
                                 reads=[et, gt], writes=[wtl])
                            S.op("pe", lambda e: e.matmul(O[:, :], lhsT=vt_[:, jj, :], rhs=wtl[:], start=first,
                                                          stop=(j == 0), skip_group_check=True),
                                 reads=[vt_, wtl], writes=[O])
                    for hi, h in enumerate(heads):
                        O = banks[4 + hi]
                        S.op("dve", lambda e: e.tensor_tensor(out=ysT[0][:, h, :], in0=O[:, :], in1=szT[:, h, :],
                                                              op=ALU.mult), reads=[O, szT], writes=[ysT[0]])

            with branch():
                qT = sb("qT", [128, NH, TG], BF16)
                szT = sb("szT", [128, NH, TG])
                w_t = [sb("w_t%d" % i, [128, TG], BF16) for i in range(4)]
                tmpA = sb("tmpA", [128, TG])
                wt = loadW(wb[l], O_XAQ, 512)
                for h in range(NH):
                    ps = nbank()
                    proj_fm(wt, h, hT, ps, ps[:, :])
                    S.op("act", lambda e: e.activation(out=qT[:, h, :], in_=ps[:, :], func=AF.Copy, scale=SCALE),
                         reads=[ps], writes=[qT])
                wt = loadW(wb[l], O_XAZ, 512)
                for h in range(NH):
                    ps = nbank()
                    proj_fm(wt, h, hT, ps, ps[:, :])
                    S.op("act", lambda e: e.activation(out=szT[:, h, :], in_=ps[:, :], func=AF.Silu),
                         reads=[ps], writes=[szT])
                for h in range(NH):
                    exs = []
                    for mc in range(2):
                        ps = nbank()
                        S.op("pe", lambda e: e.matmul(ps[:, :], lhsT=memkT[:, h, mc * 128:(mc + 1) * 128],
                                                      rhs=qT[:, h, :], start=True, stop=True),
                             reads=[memkT, qT], writes=[ps])
                        ex = rtile("w_t", 4, lambda i: w_t[i])
                        S.op("act", lambda e: e.activation(out=ex[:], in_=ps[:, :], func=AF.Exp),
                             reads=[ps], writes=[ex])
                        exs.append(ex)
                    pn = banks[0]
                    pd = banks[1]
                    for mc in range(2):
                        S.op("pe", lambda e: e.matmul(pn[:, :], lhsT=memv[:, mc, h * 128:(h + 1) * 128],
                                                      rhs=exs[mc][:], start=(mc == 0), stop=(mc == 1)),
                             reads=[memv, exs[mc]], writes=[pn])
                    for mc in range(2):
                        S.op("pe", lambda e: e.matmul(pd[:, :], lhsT=onesb[:], rhs=exs[mc][:],
                                                      start=(mc == 0), stop=(mc == 1)),
                             reads=[onesb, exs[mc]], writes=[pd])
                    S.op("dve", lambda e: e.reciprocal(out=tmpA[:], in_=pd[:, :]), reads=[pd], writes=[tmpA])
                    S.op("dve", lambda e: e.tensor_tensor(out=tmpA[:], in0=tmpA[:], in1=szT[:, h, :], op=ALU.mult),
                         reads=[tmpA, szT], writes=[tmpA])
                    S.op("dve", lambda e: e.tensor_tensor(out=ysT[4][:, h, :], in0=pn[:, :], in1=tmpA[:],
                                                          op=ALU.mult), reads=[pn, tmpA], writes=[ysT[4]])

            with branch():
                posi = sb("posi", [128, TG], I32)
                ang = sb("ang", [128, TG])
                kk = sb("kk", [128, TG])
                rr = sb("rr", [128, TG])
                cosT = sb("cosT", [128, TG])
                sinT = sb("sinT", [128, TG])
                tmpA = sb("tmpA", [128, TG])
                tmpB = sb("tmpB", [128, TG])
                qT = sb("qT", [128, NH, TG], BF16)
                kT = sb("kT", [128, NH, TG], BF16)
                qdT = sb("qdT", [128, NH, TG], BF16)
                vtok = sb("vtok", [128, 4, NH, 128], BF16)
                zsil = sb("zsil", [128, 4, W])
                kw = sb("kw", [128, 4, NH, 128], BF16)
                Rb = sb("Rb", [128, NH, 8, 128], BF16)
                scT = [sb("scT%d" % i, [128, 128], BF16) for i in range(2)]
                hm = sb("hm", [128, NH, 128])
                hsq = sb("hsq", [128, NH, 128])
                ytokb = sb("ytokb", [128, W], BF16)
                S.dma(posi[:], pos_in[0:1, t0:t0 + TG].broadcast_to([128, TG]), writes=[posi])
                S.op("dve", lambda e: e.tensor_copy(out=ang[:], in_=posi[:]), reads=[posi], writes=[ang])
                S.op("dve", lambda e: e.tensor_scalar(out=ang[:], in0=ang[:], scalar1=cview("invf"), scalar2=None,
                                                      op0=ALU.mult), reads=[ang, cf], writes=[ang])
                MAGIC = 12582912.0
                S.op("dve", lambda e: e.tensor_scalar(out=kk[:], in0=ang[:], scalar1=float(1.0 / (2 * np.pi)),
                                                      scalar2=MAGIC, op0=ALU.mult, op1=ALU.add),
                     reads=[ang], writes=[kk])
                S.op("dve", lambda e: e.tensor_scalar(out=kk[:], in0=kk[:], scalar1=MAGIC, scalar2=None,
                                                      op0=ALU.subtract), reads=[kk], writes=[kk])
                C1 = 6.28125
                C2 = float(np.float32(2 * np.pi - 6.28125))
                C3 = float(2 * np.pi - 6.28125 - np.float64(np.float32(2 * np.pi - 6.28125)))
                S.op("dve", lambda e: e.scalar_tensor_tensor(out=rr[:], in0=kk[:], scalar=-C1, in1=ang[:],
                                                             op0=ALU.mult, op1=ALU.add), reads=[kk, ang], writes=[rr])
                S.op("dve", lambda e: e.scalar_tensor_tensor(out=rr[:], in0=kk[:], scalar=-C2, in1=rr[:],
                                                             op0=ALU.mult, op1=ALU.add), reads=[kk, rr], writes=[rr])
                S.op("dve", lambda e: e.scalar_tensor_tensor(out=rr[:], in0=kk[:], scalar=-C3, in1=rr[:],
                                                             op0=ALU.mult, op1=ALU.add), reads=[kk, rr], writes=[rr])
                PI_IN = 3.1415925
                S.op("dve", lambda e: e.tensor_scalar(out=rr[:], in0=rr[:], scalar1=PI_IN, scalar2=-PI_IN,
                                                      op0=ALU.min, op1=ALU.max), reads=[rr], writes=[rr])
                S.op("dve", lambda e: e.tensor_scalar(out=kk[:], in0=rr[:], scalar1=-1.0, scalar2=None,
                                                      op0=ALU.mult), reads=[rr], writes=[kk])
                S.op("dve", lambda e: e.tensor_tensor(out=kk[:], in0=kk[:], in1=rr[:], op=ALU.min),
                     reads=[kk, rr], writes=[kk])
                S.op("dve", lambda e: e.tensor_scalar(out=kk[:], in0=kk[:], scalar1=float(np.pi / 2), scalar2=None,
                                                      op0=ALU.add), reads=[kk], writes=[kk])
                S.op("act", lambda e: e.activation(out=cosT[:], in_=kk[:], func=AF.Sin), reads=[kk], writes=[cosT])
                S.op("act", lambda e: e.activation(out=sinT[:], in_=rr[:], func=AF.Sin), reads=[rr], writes=[sinT])
                S.op("dve", lambda e: e.tensor_scalar(out=sinT[:], in0=sinT[:], scalar1=cview("sgn"), scalar2=None,
                                                      op0=ALU.mult), reads=[sinT, cf], writes=[sinT])

                def rot_proj(c0, sw0, dstT, scl):
                    wa = loadW(wb[l], c0, 512)
                    wsb = loadW(wsw[l], sw0, 512, d_wsw)
                    for h in range(NH):
                        p1 = nbank()
                        proj_fm(wa, h, hT, p1, p1[:, :])
                        S.op("dve", lambda e: e.scalar_tensor_tensor(out=tmpA[:], in0=p1[:, :], scalar=scl,
                                                                     in1=cosT[:], op0=ALU.mult, op1=ALU.mult),
                             reads=[p1, cosT], writes=[tmpA])
                        p2 = nbank()
                        proj_fm(wsb, h, hT, p2, p2[:, :])
                        S.op("dve", lambda e: e.scalar_tensor_tensor(out=tmpB[:], in0=p2[:, :], scalar=scl,
                                                                     in1=sinT[:], op0=ALU.mult, op1=ALU.mult),
                             reads=[p2, sinT], writes=[tmpB])
                        S.op("dve", lambda e: e.tensor_tensor(out=dstT[:, h, :], in0=tmpA[:], in1=tmpB[:],
                                                              op=ALU.add), reads=[tmpA, tmpB], writes=[dstT])

                rot_proj(O_RTQ, 0, qT, 1.0)
                rot_proj(O_RTK, 512, kT, SCALE)
                rwq_o = coffs["rwq"][0]
                for h in range(NH):
                    S.op("dve", lambda e: e.tensor_tensor(
                        out=qdT[:, h, :].rearrange("p (c l) -> p c l", l=64),
                        in0=qT[:, h, :].rearrange("p (c l) -> p c l", l=64),
                        in1=cf[:, rwq_o + h * 64:rwq_o + (h + 1) * 64].unsqueeze(1).broadcast_to([128, 8, 64]),
                        op=ALU.mult), reads=[qT, cf], writes=[qdT])
                wt = loadW(wb[l], O_RTV, 512)
                for i in range(4):
                    ps = nbank()
                    proj_tm(wt, i, hT, ps, ps[:, :])
                    S.op("act", lambda e: e.copy(out=vtok[:, i, :, :],
                                                 in_=ps[:, :].rearrange("p (h d) -> p h d", d=128)),
                         reads=[ps], writes=[vtok])
                wt = loadW(wb[l], O_RTZ, 512)
                for i in range(4):
                    ps = nbank()
                    proj_tm(wt, i, hT, ps, ps[:, :])
                    S.op("act", lambda e: e.activation(out=zsil[:, i, :], in_=ps[:, :], func=AF.Silu),
                         reads=[ps], writes=[zsil])
                rwe_o = coffs["rwend"][0]
                for i in range(4):
                    ps = nbank()
                    psb = ps[:].bitcast(BF16)
                    for h in range(NH):
                        S.op("pe", lambda e: e.transpose(out=psb[:, h * 128:(h + 1) * 128],
                                                         in_=kT[:, h, i * 128:(i + 1) * 128], identity=identb[:]),
                             reads=[kT, identb], writes=[ps])
                    S.op("dve", lambda e: e.tensor_tensor(
                        out=kw[:, i, :, :], in0=psb[:, 0:512].rearrange("p (h d) -> p h d", d=128),
                        in1=cf[:, rwe_o:rwe_o + 4].unsqueeze(2).broadcast_to([128, NH, 128]), op=ALU.mult),
                         reads=[ps, cf], writes=[kw])
                for c in range(8):
                    i, half_ = c // 2, (c % 2) * 64
                    for h in range(NH):
                        S.op("act", lambda e: e.copy(out=Rb[:, h, c, :], in_=Rst[:, h, :]), reads=[Rst], writes=[Rb])
                        ps = nbank()
                        S.op("pe", lambda e: e.matmul(ps[:, 0:128], lhsT=kw[half_:half_ + 64, i, h, :],
                                                      rhs=vtok[half_:half_ + 64, i, h, :], start=True, stop=True),
                             reads=[kw, vtok], writes=[ps])
                        S.op("dve", lambda e: e.scalar_tensor_tensor(out=Rst[:, h, :], in0=Rst[:, h, :],
                                                                     scalar=float(rcd[h]), in1=ps[:, 0:128],
                                                                     op0=ALU.mult, op1=ALU.add),
                             reads=[Rst, ps], writes=[Rst])
                rd_o = coffs["rdecay"][0]
                for i in range(4):
                    po = banks[0 + (i % 2)]
                    for h in range(NH):
                        ps = nbank()
                        S.op("pe", lambda e: e.matmul(ps[:, 0:128], lhsT=kT[:, h, i * 128:(i + 1) * 128],
                                                      rhs=qT[:, h, i * 128:(i + 1) * 128], start=True, stop=True),
                             reads=[kT, qT], writes=[ps])
                        sc = scT[h % 2]
                        S.op("dve", lambda e: e.tensor_tensor(out=sc[:], in0=ps[:, 0:128],
                                                              in1=cf[:, rd_o + h * 128:rd_o + (h + 1) * 128],
                                                              op=ALU.mult), reads=[ps, cf], writes=[sc])
                        S.op("pe", lambda e: e.matmul(po[:, h * 128:(h + 1) * 128], lhsT=sc[:], rhs=vtok[:, i, h, :],
                                                      start=True, stop=False, skip_group_check=True),
                             reads=[sc, vtok], writes=[po])
                        for hf in range(2):
                            c = 2 * i + hf
                            S.op("pe", lambda e: e.matmul(po[hf * 64:(hf + 1) * 64, h * 128:(h + 1) * 128],
                                                          lhsT=qdT[:, h, i * 128 + hf * 64:i * 128 + (hf + 1) * 64],
                                                          rhs=Rb[:, h, c, :], start=False, stop=(hf == 1),
                                                          skip_group_check=True),
                                 reads=[qdT, Rb], writes=[po])
                    S.op("act", lambda e: e.copy(out=hm[:], in_=po[:, :].rearrange("p (h d) -> p h d", d=128)),
                         reads=[po], writes=[hm])
                    head_norm_tok(hm, hsq)
                    S.op("dve", lambda e: e.tensor_tensor(out=ytokb[:], in0=hm[:].rearrange("p h d -> p (h d)"),
                                                          in1=zsil[:, i, :], op=ALU.mult), reads=[hm, zsil],
                         writes=[ytokb])
                    tok_to_fm(ytokb, ysT[3], i)

            with branch():
                gu = sb("gu", [128, 4, W])
                gv = sb("gv", [128, 4, W])
                zsil = sb("zsil", [128, 4, W])
                gvb = sb("gvb", [128, W], BF16)
                ytok = sb("ytok", [128, W])
                ytokb = sb("ytokb", [128, W], BF16)
                tmpA = sb("tmpA", [128, TG])
                tmpB = sb("tmpB", [128, TG])
                junk = sb("junk", [128, W])
                wt = loadW(wb[l], O_GMU, 512)
                for i in range(4):
                    ps = nbank()
                    proj_tm(wt, i, hT, ps, ps[:, :])
                    gelu_tanh(gu, gu[:, i, :], ps, ps[:, :], tmpA, tmpB)
                wt = loadW(wb[l], O_GMV, 512)
                for i in range(4):
                    ps = nbank()
                    proj_tm(wt, i, hT, ps, ps[:, :])
                    gelu_tanh(gv, gv[:, i, :], ps, ps[:, :], tmpA, tmpB)
                wt = loadW(wb[l], O_GMZ, 512)
                for i in range(4):
                    ps = nbank()
                    proj_tm(wt, i, hT, ps, ps[:, :])
                    S.op("act", lambda e: e.activation(out=zsil[:, i, :], in_=ps[:, :], func=AF.Silu),
                         reads=[ps], writes=[zsil])
                for i in range(4):
                    S.op("dve", lambda e: e.tensor_reduce(out=st1[:, 0:1], in_=gv[:, i, :], axis=AX.X, op=ALU.add),
                         reads=[gv], writes=[st1])
                    S.op("act", lambda e: e.activation(out=junk[:], in_=gv[:, i, :], func=AF.Square,
                                                       accum_out=st2[:, 0:1]), reads=[gv], writes=[junk, st2])
                    S.op("dve", lambda e: e.tensor_scalar(out=st1[:, 0:1], in0=st1[:, 0:1], scalar1=1.0 / W,
                                                          scalar2=None, op0=ALU.mult), reads=[st1], writes=[st1])
                    S.op("dve", lambda e: e.tensor_tensor(out=st3[:, 0:1], in0=st1[:, 0:1], in1=st1[:, 0:1],
                                                          op=ALU.mult), reads=[st1], writes=[st3])
                    S.op("dve", lambda e: e.scalar_tensor_tensor(out=st2[:, 0:1], in0=st2[:, 0:1], scalar=1.0 / W,
                                                                 in1=st3[:, 0:1], op0=ALU.mult, op1=ALU.subtract),
                         reads=[st2, st3], writes=[st2])
                    rsqrt_cols(st3, st2, 1, 1.0)
                    S.op("dve", lambda e: e.tensor_scalar(out=ytok[:], in0=gv[:, i, :], scalar1=st1[:, 0:1],
                                                          scalar2=st3[:, 0:1], op0=ALU.subtract, op1=ALU.mult),
                         reads=[gv, st1, st3], writes=[ytok])
                    S.op("dve", lambda e: e.tensor_tensor(out=ytok[:], in0=ytok[:], in1=lng[:], op=ALU.mult),
                         reads=[ytok, lng], writes=[ytok])
                    S.op("dve", lambda e: e.tensor_tensor(out=gvb[:], in0=ytok[:], in1=lnb[:], op=ALU.add),
                         reads=[ytok, lnb], writes=[gvb])
                    ps = nbank()
                    for g_ in range(4):
                        S.op("pe", lambda e: e.matmul(ps[:, g_ * 128:(g_ + 1) * 128], lhsT=gwT[:, g_, :],
                                                      rhs=gvb[:, g_ * 128:(g_ + 1) * 128], start=True, stop=True,
                                                      skip_group_check=True),
                             reads=[gwT, gvb], writes=[ps])
                    for g_ in range(4):
                        S.op("dve", lambda e: e.scalar_tensor_tensor(
                            out=ytok[:, g_ * 128:(g_ + 1) * 128], in0=ps[:, g_ * 128:(g_ + 1) * 128],
                            scalar=gbs[:, g_:g_ + 1], in1=gu[:, i, g_ * 128:(g_ + 1) * 128], op0=ALU.add,
                            op1=ALU.mult), reads=[ps, gbs, gu], writes=[ytok])
                    S.op("dve", lambda e: e.tensor_tensor(out=ytokb[:], in0=ytok[:], in1=zsil[:, i, :], op=ALU.mult),
                         reads=[ytok, zsil], writes=[ytokb])
                    tok_to_fm(ytokb, ysT[2], i)

            with branch():
                pre = sb("pre", [128, NH, TG + 3])
                cva = sb("cva", [128, NH, TG])
                qT = sb("qT", [128, NH, TG], BF16)
                kT = sb("kT", [128, NH, TG], BF16)
                qdT = sb("qdT", [128, NH, TG], BF16)
                kw = sb("kw", [128, 4, NH, 128], BF16)
                vaug = sb("vaug", [128, 4, NH, 129], BF16)
                oz = sb("oz", [128, 4, W])
                Cb = sb("Cb", [128, NH, 8, 129], BF16)
                Ctmp = sb("Ctmp", [128, 129])
                R = {n: sb("row_" + n, [4, TG]) for n in
                     ["i", "sp", "bcum", "a", "g", "bm", "mt", "dint", "r1", "emt", "wend"]}
                mseq = sb("mseq", [4, 8])
                mprev = sb("mprev", [4, 8])
                dlt = sb("dlt", [4, 8])
                dold_loc = sb("dold_loc", [4, 16])
                colsb = sb("colsb", [128, 4, 2, 4])
                dcol = sb("dcol", [128, NH, 16])
                Dm = [sb("Dm%d" % i, [128, 128]) for i in range(2)]
                scT = [sb("scT%d" % i, [128, 128], BF16) for i in range(2)]
                hm = sb("hm", [128, NH, 128])
                hsq = sb("hsq", [128, NH, 128])
                ytok = sb("ytok", [128, W])
                ytokb = sb("ytokb", [128, W], BF16)

                def conv_branch(c0, hidx, cw, dstT, scl):
                    wt_ = loadW(wb[l], c0, 512)
                    S.op("pool", lambda e: e.tensor_copy(out=pre[:, :, 0:3], in_=halo[:, hidx, :, :]),
                         reads=[halo], writes=[pre])
                    for h in range(NH):
                        ps = nbank()
                        proj_fm(wt_, h, hT, ps, ps[:, :])
                        S.op("act", lambda e: e.copy(out=pre[:, h, 3:TG + 3], in_=ps[:, :]), reads=[ps], writes=[pre])
                    S.op("pool", lambda e: e.tensor_copy(out=halo[:, hidx, :, :], in_=pre[:, :, TG:TG + 3]),
                         reads=[pre], writes=[halo])
                    for h in range(NH):
                        S.op("dve", lambda e: e.tensor_scalar(out=cva[:, h, :], in0=pre[:, h, 0:TG],
                                                              scalar1=cw[:, h, 0:1], scalar2=None, op0=ALU.mult),
                             reads=[pre, cw], writes=[cva])
                        for tap in range(1, 4):
                            S.op("dve", lambda e: e.scalar_tensor_tensor(
                                out=cva[:, h, :], in0=pre[:, h, tap:tap + TG], scalar=cw[:, h, tap:tap + 1],
                                in1=cva[:, h, :], op0=ALU.mult, op1=ALU.add), reads=[pre, cw, cva], writes=[cva])
                    S.op("act", lambda e: e.activation(out=cva[:], in_=cva[:], func=AF.Silu), reads=[cva], writes=[cva])
                    S.op("dve", lambda e: e.tensor_scalar(out=dstT[:], in0=cva[:], scalar1=scl, scalar2=None,
                                                          op0=ALU.mult), reads=[cva], writes=[dstT])

                conv_branch(O_MLQ, 0, cq, qT, 1.0)
                conv_branch(O_MLK, 1, ck, kT, SCALE)
                wt = loadW(wb[l], O_MLV, 512)
                for i in range(4):
                    ps = nbank()
                    proj_tm(wt, i, hT, ps, ps[:, :])
                    S.op("act", lambda e: e.copy(out=vaug[:, i, :, 0:128],
                                                 in_=ps[:, :].rearrange("p (h d) -> p h d", d=128)),
                         reads=[ps], writes=[vaug])
                S.op("dve", lambda e: e.memset(vaug[:, :, :, 128:129], 1.0), writes=[vaug])
                wt = loadW(wb[l], O_MLO, 512)
                for i in range(4):
                    ps = nbank()
                    proj_tm(wt, i, hT, ps, ps[:, :])
                    S.op("act", lambda e: e.activation(out=oz[:, i, :], in_=ps[:, :], func=AF.Sigmoid),
                         reads=[ps], writes=[oz])
                wt = loadW(wb[l], O_MLZ, 512)
                for i in range(4):
                    ps = nbank()
                    proj_tm(wt, i, hT, ps, ps[:, :])
                    S.op("act", lambda e: e.activation(out=ytok[:], in_=ps[:, :], func=AF.Silu),
                         reads=[ps], writes=[ytok])
                    S.op("dve", lambda e: e.tensor_tensor(out=oz[:, i, :], in0=oz[:, i, :], in1=ytok[:], op=ALU.mult),
                         reads=[oz, ytok], writes=[oz])
                wt = loadW(wb[l], O_MLI, 8)
                ps = nbank()
                mm_acc(ps, ps[0:4, :], [(wt[:, kc, 0:4], hT[:, kc, :]) for kc in range(8)], extra_reads=[wt, hT])
                S.op("act", lambda e: e.activation(out=R["i"][:], in_=ps[0:4, :], func=AF.Identity, bias=big[:, 0:1]),
                     reads=[ps, big], writes=[R["i"]])
                ps = nbank()
                mm_acc(ps, ps[0:4, :], [(wt[:, kc, 4:8], hT[:, kc, :]) for kc in range(8)], extra_reads=[wt, hT])
                S.op("act", lambda e: e.activation(out=R["sp"][:], in_=ps[0:4, :], func=AF.Exp, scale=-1.0,
                                                   bias=nbfg[:, 0:1]), reads=[ps, nbfg], writes=[R["sp"]])
                S.op("act", lambda e: e.activation(out=R["sp"][:], in_=R["sp"][:], func=AF.Ln, bias=1.0),
                     reads=[R["sp"]], writes=[R["sp"]])
                rm_o = coffs["resetm"][0]
                nr_o = coffs["negreset"][0]
                S.op("dve", lambda e: e.tensor_tensor_scan(out=R["bcum"][:], data0=cf[0:4, rm_o:rm_o + TG],
                                                           data1=R["sp"][:], initial=0.0, op0=ALU.mult,
                                                           op1=ALU.subtract), reads=[cf, R["sp"]], writes=[R["bcum"]])
                S.op("dve", lambda e: e.tensor_tensor(out=R["a"][:], in0=R["i"][:], in1=R["bcum"][:],
                                                      op=ALU.subtract), reads=[R["i"], R["bcum"]], writes=[R["a"]])
                S.op("dve", lambda e: e.tensor_tensor_scan(out=R["g"][:], data0=cf[0:4, nr_o:nr_o + TG],
                                                           data1=R["a"][:], initial=-1e30, op0=ALU.add, op1=ALU.max),
                     reads=[cf, R["a"]], writes=[R["g"]])
                S.op("dve", lambda e: e.tensor_tensor(out=R["g"][:], in0=R["bcum"][:], in1=R["g"][:], op=ALU.add),
                     reads=[R["bcum"], R["g"]], writes=[R["g"]])
                blast = R["bcum"][:].rearrange("p (c l) -> p c l", l=64)[:, :, 63]
                alast = R["g"][:].rearrange("p (c l) -> p c l", l=64)[:, :, 63]
                S.op("dve", lambda e: e.tensor_tensor_scan(out=mseq[:], data0=blast, data1=alast,
                                                           initial=mcar[:, 0:1], op0=ALU.add, op1=ALU.max),
                     reads=[R["bcum"], R["g"], mcar], writes=[mseq])
                S.op("dve", lambda e: e.tensor_copy(out=mprev[:, 0:1], in_=mcar[:, 0:1]), reads=[mcar], writes=[mprev])
                S.op("dve", lambda e: e.tensor_copy(out=mprev[:, 1:8], in_=mseq[:, 0:7]), reads=[mseq], writes=[mprev])
                S.op("dve", lambda e: e.tensor_copy(out=mcar[:, 0:1], in_=mseq[:, 7:8]), reads=[mseq], writes=[mcar])
                v3 = lambda t_: t_[:].rearrange("p (c l) -> p c l", l=64)
                bc8 = lambda t_: t_[:, :].unsqueeze(2).broadcast_to([4, 8, 64])
                S.op("dve", lambda e: e.tensor_tensor(out=v3(R["bm"]), in0=v3(R["bcum"]), in1=bc8(mprev), op=ALU.add),
                     reads=[R["bcum"], mprev], writes=[R["bm"]])
                S.op("dve", lambda e: e.tensor_tensor(out=R["mt"][:], in0=R["bm"][:], in1=R["g"][:], op=ALU.max),
                     reads=[R["bm"], R["g"]], writes=[R["mt"]])
                S.op("dve", lambda e: e.tensor_tensor(out=R["bm"][:], in0=R["bm"][:], in1=R["mt"][:],
                                                      op=ALU.subtract), reads=[R["bm"], R["mt"]], writes=[R["bm"]])
                S.op("act", lambda e: e.activation(out=R["dint"][:], in_=R["bm"][:], func=AF.Exp),
                     reads=[R["bm"]], writes=[R["dint"]])
                S.op("dve", lambda e: e.tensor_tensor(out=R["r1"][:], in0=R["bcum"][:], in1=R["mt"][:],
                                                      op=ALU.subtract), reads=[R["bcum"], R["mt"]], writes=[R["r1"]])
                S.op("act", lambda e: e.activation(out=R["emt"][:], in_=R["mt"][:], func=AF.Exp, scale=-1.0),
                     reads=[R["mt"]], writes=[R["emt"]])
                S.op("dve", lambda e: e.tensor_tensor(out=dlt[:], in0=blast, in1=alast, op=ALU.subtract),
                     reads=[R["bcum"], R["g"]], writes=[dlt])
                S.op("dve", lambda e: e.tensor_tensor(out=v3(R["bm"]), in0=v3(R["a"]), in1=bc8(dlt), op=ALU.add),
                     reads=[R["a"], dlt], writes=[R["bm"]])
                S.op("act", lambda e: e.activation(out=R["wend"][:], in_=R["bm"][:], func=AF.Exp),
                     reads=[R["bm"]], writes=[R["wend"]])
                S.op("dve", lambda e: e.tensor_tensor(out=dold_loc[:, 0:8], in0=blast, in1=mprev[:], op=ALU.add),
                     reads=[R["bcum"], mprev], writes=[dold_loc])
                S.op("dve", lambda e: e.tensor_tensor(out=dold_loc[:, 0:8], in0=dold_loc[:, 0:8], in1=mseq[:],
                                                      op=ALU.subtract), reads=[dold_loc, mseq], writes=[dold_loc])
                S.op("dve", lambda e: e.tensor_tensor(out=dold_loc[:, 8:16], in0=alast, in1=mseq[:], op=ALU.subtract),
                     reads=[R["g"], mseq], writes=[dold_loc])
                S.op("act", lambda e: e.activation(out=dold_loc[:], in_=dold_loc[:], func=AF.Exp),
                     reads=[dold_loc], writes=[dold_loc])
                ps = nbank()
                for i in range(4):
                    for qi, nm in enumerate(("emt", "wend")):
                        o_ = (i * 2 + qi) * 4
                        S.op("pe", lambda e: e.matmul(ps[:, o_:o_ + 4], lhsT=R[nm][:, i * 128:(i + 1) * 128], rhs=i4v,
                                                      start=True, stop=True, skip_group_check=True),
                             reads=[R[nm], cf], writes=[ps])
                S.op("dve", lambda e: e.tensor_copy(out=colsb[:].rearrange("p a b c -> p (a b c)"), in_=ps[:, 0:32]),
                     reads=[ps], writes=[colsb])
                ps = nbank()
                for h in range(NH):
                    S.op("pe", lambda e: e.matmul(ps[:, h * 16:(h + 1) * 16], lhsT=selv(h), rhs=dold_loc[:],
                                                  start=True, stop=True, skip_group_check=True),
                         reads=[cf, dold_loc], writes=[ps])
                S.op("dve", lambda e: e.tensor_copy(out=dcol[:].rearrange("p h c -> p (h c)"), in_=ps[:, 0:64]),
                     reads=[ps], writes=[dcol])
                for h in range(NH):
                    ps = nbank()
                    S.op("pe", lambda e: e.matmul(ps[:, :], lhsT=selv(h), rhs=R["dint"][:], start=True, stop=True),
                         reads=[cf, R["dint"]], writes=[ps])
                    S.op("dve", lambda e: e.tensor_tensor(out=qdT[:, h, :], in0=ps[:, :], in1=qT[:, h, :],
                                                          op=ALU.mult), reads=[ps, qT], writes=[qdT])
                for i in range(4):
                    ps = nbank()
                    psb = ps[:].bitcast(BF16)
                    for h in range(NH):
                        S.op("pe", lambda e: e.transpose(out=psb[:, h * 128:(h + 1) * 128],
                                                         in_=kT[:, h, i * 128:(i + 1) * 128], identity=identb[:]),
                             reads=[kT, identb], writes=[ps])
                    S.op("dve", lambda e: e.tensor_tensor(
                        out=kw[:, i, :, :], in0=psb[:, 0:512].rearrange("p (h d) -> p h d", d=128),
                        in1=colsb[:, i, 1, :].unsqueeze(2).broadcast_to([128, NH, 128]), op=ALU.mult),
                         reads=[ps, colsb], writes=[kw])
                for c in range(8):
                    i, half_ = c // 2, (c % 2) * 64
                    for h in range(NH):
                        S.op("act", lambda e: e.copy(out=Cb[:, h, c, :], in_=Cst[:, h, :]), reads=[Cst], writes=[Cb])
                        ps = nbank()
                        S.op("pe", lambda e: e.matmul(ps[:, 0:129], lhsT=kw[half_:half_ + 64, i, h, :],
                                                      rhs=vaug[half_:half_ + 64, i, h, :], start=True, stop=True),
                             reads=[kw, vaug], writes=[ps])
                        S.op("dve", lambda e: e.tensor_scalar(out=Ctmp[:], in0=Cst[:, h, :],
                                                              scalar1=dcol[:, h, c:c + 1], scalar2=None,
                                                              op0=ALU.mult), reads=[Cst, dcol], writes=[Ctmp])
                        S.op("dve", lambda e: e.scalar_tensor_tensor(out=Cst[:, h, :], in0=ps[:, 0:129],
                                                                     scalar=dcol[:, h, 8 + c:9 + c], in1=Ctmp[:],
                                                                     op0=ALU.mult, op1=ALU.add),
                             reads=[ps, dcol, Ctmp], writes=[Cst])
                for i in range(4):
                    pA = [banks[4], banks[5]]
                    for h in range(NH):
                        ps = banks[0 + (h % 2)]
                        S.op("pe", lambda e: e.matmul(ps[:, 0:128], lhsT=kT[:, h, i * 128:(i + 1) * 128],
                                                      rhs=qT[:, h, i * 128:(i + 1) * 128], start=True, stop=True),
                             reads=[kT, qT], writes=[ps])
                        pd_ = banks[2 + (h % 2)]
                        S.op("pe", lambda e: e.matmul(pd_[:, 0:128], lhsT=selv(h),
                                                      rhs=R["r1"][:, i * 128:(i + 1) * 128],
                                                      start=True, stop=False), reads=[cf, R["r1"]], writes=[pd_])
                        S.op("pe", lambda e: e.matmul(pd_[:, 0:128], lhsT=R["a"][:, i * 128:(i + 1) * 128],
                                                      rhs=selv(h), start=False, stop=False),
                             reads=[cf, R["a"]], writes=[pd_])
                        S.op("pe", lambda e: e.matmul(pd_[:, 0:128], lhsT=identb[:], rhs=mlmaskb[:], start=False,
                                                      stop=True), reads=[identb, mlmaskb], writes=[pd_])
                        dm = Dm[h % 2]
                        S.op("act", lambda e: e.activation(out=dm[:], in_=pd_[:, 0:128], func=AF.Exp),
                             reads=[pd_], writes=[dm])
                        sc = scT[h % 2]
                        S.op("dve", lambda e: e.tensor_tensor(out=sc[:], in0=ps[:, 0:128], in1=dm[:], op=ALU.mult),
                             reads=[ps, dm], writes=[sc])
                        po = pA[h // 2]
                        oo = (h % 2) * 129
                        S.op("pe", lambda e: e.matmul(po[:, oo:oo + 129], lhsT=sc[:], rhs=vaug[:, i, h, :],
                                                      start=True, stop=False, skip_group_check=True),
                             reads=[sc, vaug], writes=[po])
                        for hf in range(2):
                            c = 2 * i + hf
                            S.op("pe", lambda e: e.matmul(po[hf * 64:(hf + 1) * 64, oo:oo + 129],
                                                          lhsT=qdT[:, h, i * 128 + hf * 64:i * 128 + (hf + 1) * 64],
                                                          rhs=Cb[:, h, c, :], start=False, stop=(hf == 1),
                                                          skip_group_check=True),
                                 reads=[qdT, Cb], writes=[po])
                    for hp in range(2):
                        po = pA[hp]
                        pv = po[:, 0:258].rearrange("p (h e) -> p h e", e=129)
                        S.op("dve", lambda e: e.tensor_copy(out=den4[:, 2 * hp:2 * hp + 2], in_=pv[:, :, 128]),
                             reads=[po], writes=[den4])
                    S.op("dve", lambda e: e.tensor_scalar(out=st1[:], in0=den4[:], scalar1=-1.0, scalar2=None,
                                                          op0=ALU.mult), reads=[den4], writes=[st1])
                    S.op("dve", lambda e: e.tensor_tensor(out=den4[:], in0=den4[:], in1=st1[:], op=ALU.max),
                         reads=[den4, st1], writes=[den4])
                    S.op("dve", lambda e: e.tensor_tensor(out=den4[:], in0=den4[:], in1=colsb[:, i, 0, :],
                                                          op=ALU.max), reads=[den4, colsb], writes=[den4])
                    S.op("dve", lambda e: e.reciprocal(out=den4[:], in_=den4[:]), reads=[den4], writes=[den4])
                    for hp in range(2):
                        po = pA[hp]
                        pv = po[:, 0:258].rearrange("p (h e) -> p h e", e=129)
                        S.op("dve", lambda e: e.tensor_tensor(
                            out=hm[:, 2 * hp:2 * hp + 2, :], in0=pv[:, :, 0:128],
                            in1=den4[:, 2 * hp:2 * hp + 2].unsqueeze(2).broadcast_to([128, 2, 128]), op=ALU.mult),
                             reads=[po, den4], writes=[hm])
                    head_norm_tok(hm, hsq)
                    S.op("dve", lambda e: e.tensor_tensor(out=ytokb[:], in0=hm[:].rearrange("p h d -> p (h d)"),
                                                          in1=oz[:, i, :], op=ALU.mult), reads=[hm, oz],
                         writes=[ytokb])
                    tok_to_fm(ytokb, ysT[1], i)

            with branch():
                sig = [sb("sig%d" % i, [128, TG]) for i in range(5)]
                mrg = sb("mrg", [128, 8, TG], BF16)
                macc = [sb("macc%d" % i, [128, TG]) for i in range(2)]
                otk = [sb("otk%d" % i, [128, D]) for i in range(4)]
                xr = [sb("xr%d" % i, [128, D]) for i in range(2)]
                junk = sb("junk", [128, D])
                for dc in range(8):
                    for n in range(5):
                        wt = loadW(wb[l], O_GATES + n * 1024 + dc * 128, 128)
                        ps = nbank()
                        proj_fm(wt, 0, hT, ps, ps[:, :])
                        S.op("act", lambda e: e.activation(out=sig[n][:], in_=ps[:, :], func=AF.Sigmoid),
                             reads=[ps], writes=[sig[n]])
                    for n in range(5):
                        t = Wt[wt_rr[0] % len(Wt)]
                        wt_rr[0] += 1
                        S.dma(t[:, 0:4, 0:128],
                              wupb[l][n * W:(n + 1) * W, dc * 128:(dc + 1) * 128].rearrange("(c p) n -> p c n", p=128),
                              reads=[d_wupb], writes=[t])
                        ps = nbank()
                        mm_acc(ps, ps[:, :], [(t[:, c, 0:128], ysT[n][:, c, :]) for c in range(4)],
                               extra_reads=[t, ysT[n]])
                        if n == 0:
                            S.op("dve", lambda e: e.tensor_tensor(out=macc[0][:], in0=ps[:, :], in1=sig[n][:],
                                                                  op=ALU.mult), reads=[ps, sig[n]], writes=[macc[0]])
                        else:
                            S.op("dve", lambda e: e.tensor_tensor(out=macc[1][:], in0=ps[:, :], in1=sig[n][:],
                                                                  op=ALU.mult), reads=[ps, sig[n]], writes=[macc[1]])
                            if n < 4:
                                S.op("pool", lambda e: e.tensor_tensor(out=macc[0][:], in0=macc[0][:],
                                                                       in1=macc[1][:], op=ALU.add),
                                     reads=[macc[0], macc[1]], writes=[macc[0]])
                            else:
                                S.op("pool", lambda e: e.tensor_tensor(out=mrg[:, dc, :], in0=macc[0][:],
                                                                       in1=macc[1][:], op=ALU.add),
                                     reads=[macc[0], macc[1]], writes=[mrg])
                for hc in range(2):
                    wt = loadW(woutb[l], hc * 512, 512, d_woutb)
                    for i in range(4):
                        ps = nbank()
                        proj_tm(wt, i, mrg, ps, ps[:, :])
                        S.op("act", lambda e: e.copy(out=otk[i][:, hc * 512:(hc + 1) * 512], in_=ps[:, :]),
                             reads=[ps], writes=[otk[i]])
                for i in range(4):
                    S.op("act", lambda e: e.activation(out=junk[:], in_=otk[i][:], func=AF.Square,
                                                       accum_out=ss4[:, i:i + 1]),
                         reads=[otk[i]], writes=[junk, ss4])
                rsqrt_cols(rstd4, ss4, 4, 1.0 / D)
                for i in range(4):
                    x_ = xr[i % 2]
                    S.dma(x_[:], xsrc[t0 + i * 128:t0 + (i + 1) * 128, :],
                          reads=[xsrc_t] if xsrc_t is not None else [], writes=[x_])
                    S.op("dve", lambda e: e.scalar_tensor_tensor(out=otk[i][:], in0=otk[i][:],
                                                                 scalar=rstd4[:, i:i + 1], in1=gpost[:],
                                                                 op0=ALU.mult, op1=ALU.mult),
                         reads=[otk[i], rstd4, gpost], writes=[otk[i]])
                    S.op("pool", lambda e: e.tensor_tensor(out=otk[i][:], in0=otk[i][:], in1=x_[:], op=ALU.add),
                         reads=[otk[i], x_], writes=[otk[i]])
                    S.dma(xdst[t0 + i * 128:t0 + (i + 1) * 128, :], otk[i][:], reads=[otk[i]], writes=[xdst_t])
    S.finish()
    print("instructions:", S.ninst, "sems:", S.nsem, "sbuf max bytes/partition:", max_bytes[0])
    return nc, carr, amask_np


_CACHE = {}


def kernel(**inputs):
    TT = inputs["x"].shape[1]
    if TT not in _CACHE:
        _CACHE[TT] = build(TT)
    nc, carr, amask_np = _CACHE[TT]
    m = {}
    for k, v in inputs.items():
        a = np.asarray(v)
        if k in ("x", "mem"):
            a = a[0]
        m[k] = np.ascontiguousarray(a)
    m["cst"] = carr
    m["cst_am"] = amask_np
    res = run_bass_kernel_spmd(nc, [dict(m) for _ in range(8)], core_ids=list(range(8)))
    y = np.asarray(res.results[0]["y"], dtype=np.float32)
    return y[None, :, :]
```

```python
import numpy as np
import concourse.bass as bass
import concourse.mybir as mybir
from concourse.bass_utils import run_bass_kernel_spmd

F32 = mybir.dt.float32
BF16 = mybir.dt.bfloat16
I32 = mybir.dt.int32
AF = mybir.ActivationFunctionType
ALU = mybir.AluOpType
AX = mybir.AxisListType

D = 1024
W = 512
NH = 4
HD = 128
DEPTH = 2
NMEM = 256
DIN = 14344
EPS = 1e-6
TG = 512
O_SBQ, O_SBK, O_SBV, O_SBZ = 0, 512, 1024, 1536
O_MLQ, O_MLK, O_MLV, O_MLO, O_MLZ, O_MLI, O_MLF = 2048, 2560, 3072, 3584, 4096, 4608, 4612
O_GMU, O_GMV, O_GMZ = 4616, 5128, 5640
O_RTQ, O_RTK, O_RTV, O_RTZ = 6152, 6664, 7176, 7688
O_XAQ, O_XAZ = 8200, 8712
O_GATES = 9224
NEG = -30000.0
SCALE = HD ** -0.5


class T:
    __slots__ = ("t", "w", "r", "name")

    def __init__(self, t, name=""):
        self.t = t
        self.w = None
        self.r = {}
        self.name = name

    def __getitem__(self, idx):
        return self.t[idx]


class Sched:
    EPOCH = 30000

    def __init__(self, nc, ndma_sems=24):
        self.nc = nc
        self.eng = {"pe": nc.tensor, "act": nc.scalar, "dve": nc.vector, "pool": nc.gpsimd, "sp": nc.sync}
        self.sem = {}
        self.cnt = {}
        self.seen = {e: {} for e in self.eng}
        self.nsem = 0
        for e in ("pe", "act", "dve", "pool"):
            self.sem[e] = self._newsem(e)
            self.cnt[e] = 0
        self.dma_sems = [self._newsem("dma%d" % i) for i in range(ndma_sems)]
        self.dma_cnt = [0] * ndma_sems
        self.dma_rr = 0
        self.ninst = 0

    def _newsem(self, name):
        self.nsem += 1
        return self.nc.semaphore("s_%s_%d" % (name, self.nsem)).__enter__()

    def _need(self, e, ev):
        sem, val = ev
        key = id(sem)
        if self.seen[e].get(key, 0) < val:
            self.eng[e].wait_ge(sem, val)
            self.seen[e][key] = val

    def _deps(self, e, reads, writes):
        for t in reads:
            if t.w is not None and not (e == "pe" and t.w[2] == "pe"):
                self._need(e, t.w[:2])
        for t in writes:
            if t.w is not None and not (e == "pe" and t.w[2] == "pe"):
                self._need(e, t.w[:2])
            for (re_, ev) in t.r.items():
                if not (e == "pe" and re_ == "pe"):
                    self._need(e, ev)

    def _commit(self, e, ev, reads, writes):
        for t in reads:
            t.r[e] = ev
        for t in writes:
            t.w = (ev[0], ev[1], e)
            t.r = {}

    def op(self, e, fn, reads=(), writes=()):
        self._deps(e, reads, writes)
        if self.cnt[e] >= self.EPOCH:
            self.sem[e] = self._newsem(e)
            self.cnt[e] = 0
        ins = fn(self.eng[e])
        self.cnt[e] += 1
        ins.then_inc(self.sem[e], 1)
        ev = (self.sem[e], self.cnt[e])
        self._commit(e, ev, reads, writes)
        self.ninst += 1
        return ins

    def dma(self, out, in_, reads=(), writes=(), q="sp", **kw):
        k = self.dma_rr
        self.dma_rr = (k + 1) % len(self.dma_sems)
        sem = self.dma_sems[k]
        if self.dma_cnt[k] > 0:
            self._need(q, (sem, self.dma_cnt[k]))
        self._deps(q, reads, writes)
        self.dma_cnt[k] += 16
        self.eng[q].dma_start(out=out, in_=in_, **kw).then_inc(sem, 16)
        ev = (sem, self.dma_cnt[k])
        self._commit("dma%d" % k, ev, reads, writes)
        self.ninst += 1
        return ev

    def barrier(self):
        evs = [(self.sem[e], self.cnt[e]) for e in ("pe", "act", "dve", "pool") if self.cnt[e] > 0]
        evs += [(self.dma_sems[k], self.dma_cnt[k]) for k in range(len(self.dma_sems)) if self.dma_cnt[k] > 0]
        for e in self.eng:
            for ev in evs:
                if not (e in self.sem and ev[0] is self.sem[e]):
                    self._need(e, ev)

    def finish(self):
        for k in range(len(self.dma_sems)):
            if self.dma_cnt[k] > 0:
                self._need("sp", (self.dma_sems[k], self.dma_cnt[k]))


def host_consts():
    c = {}
    i = np.arange(128)
    c["ident"] = np.eye(128, dtype=np.float32)
    c["negU"] = -(i[:, None] >= i[None, :]).astype(np.float32)
    c["negL"] = -(i[:, None] < i[None, :]).astype(np.float32)
    q = np.arange(512)
    am = np.zeros((128, 4, 512), np.float32)
    for a in range(4):
        am[:, a, :] = np.where(i[:, None] + 128 * a >= q[None, :], NEG, 0.0)
    c["amask"] = am.reshape(128, 2048)
    same = (i[:, None] // 64) == (i[None, :] // 64)
    c["mlmask"] = np.where(same & (i[:, None] <= i[None, :]), 0.0, NEG).astype(np.float32)
    lg = np.log(np.float32(1.0) - np.float32(2.0) ** (-5.0 - np.arange(4, dtype=np.float32))).astype(np.float32)
    dec = np.zeros((128, 4, 128), np.float32)
    for h in range(4):
        dec[:, h, :] = np.where(same, np.exp(lg[h] * np.abs(i[:, None] - i[None, :]).astype(np.float32)), 0.0)
    c["rdecay"] = dec.reshape(128, 512)
    wq = np.zeros((128, 4, 64), np.float32)
    for h in range(4):
        wq[:, h, :] = np.exp(lg[h] * (np.arange(64).astype(np.float32) + 1.0))[None, :]
    c["rwq"] = wq.reshape(128, 256)
    we = np.zeros((128, 4), np.float32)
    for h in range(4):
        we[:, h] = np.exp(lg[h] * (63.0 - (i % 64).astype(np.float32)))
    c["rwend"] = we
    c["rcd"] = np.exp(lg * np.float32(64.0)).astype(np.float32)
    sel = np.zeros((128, 4, 128), np.float32)
    for h in range(4):
        sel[h, h, :] = 1.0
    c["sel"] = sel.reshape(128, 512)
    i4 = np.zeros((128, 4), np.float32)
    i4[:4, :4] = np.eye(4)
    c["i4"] = i4
    rm = np.ones((128, 512), np.float32)
    rm[:, ::64] = 0.0
    c["resetm"] = rm
    nr = np.zeros((128, 512), np.float32)
    nr[:, ::64] = -1e30
    c["negreset"] = nr
    half = 64
    invf = (np.float32(10000.0) ** (-np.arange(half, dtype=np.float32) / np.float32(half))).astype(np.float32)
    c["invf"] = np.concatenate([invf, invf])[:, None].astype(np.float32)
    c["sgn"] = np.concatenate([-np.ones(64), np.ones(64)])[:, None].astype(np.float32)
    gmm = np.ones((128, 128), np.float32)
    gmm[64:, :64] = 0.0
    c["gmmask"] = gmm
    names = ["ident", "negU", "negL", "mlmask", "rdecay", "rwq", "rwend", "sel", "i4", "resetm",
             "negreset", "invf", "sgn", "gmmask"]
    offs = {}
    o = 0
    for n in names:
        offs[n] = (o, c[n].shape[1])
        o += c[n].shape[1]
    arr = np.concatenate([c[n] for n in names], axis=1).astype(np.float32)
    return arr, offs, c["rcd"], c["amask"]


def build(TT):
    NG = TT // TG
    NTILE = TT // 128
    carr, coffs, rcd, amask_np = host_consts()
    NCF = carr.shape[1]
    nc = bass.Bass("TRN2", target_bir_lowering=False)
    S = Sched(nc)
    sbytes = [0]

    def dram(name, shape, dt, kind):
        return nc.dram_tensor(name, shape, dt, kind=kind).ap()

    x_in = dram("x", [TT, D], F32, "ExternalInput")
    mem_in = dram("mem", [NMEM, D], F32, "ExternalInput")
    pos_in = dram("positions", [1, TT], I32, "ExternalInput")
    norm_pre = dram("norm_pre", [DEPTH, D], F32, "ExternalInput")
    norm_post = dram("norm_post", [DEPTH, D], F32, "ExternalInput")
    w_in = dram("w_in", [DEPTH, D, DIN], F32, "ExternalInput")
    b_ig = dram("b_igate", [DEPTH, NH], F32, "ExternalInput")
    b_fg = dram("b_fgate", [DEPTH, NH], F32, "ExternalInput")
    conv_q = dram("conv_q", [DEPTH, 4, W], F32, "ExternalInput")
    conv_k = dram("conv_k", [DEPTH, 4, W], F32, "ExternalInput")
    gm_ln_g = dram("gm_ln_g", [DEPTH, W], F32, "ExternalInput")
    gm_ln_b = dram("gm_ln_b", [DEPTH, W], F32, "ExternalInput")
    gm_ws = dram("gm_ws", [DEPTH, 4, 128, 128], F32, "ExternalInput")
    gm_bs = dram("gm_bs", [DEPTH, 4, 128], F32, "ExternalInput")
    mem_norm = dram("mem_norm", [DEPTH, D], F32, "ExternalInput")
    w_mem_kv = dram("w_mem_kv", [DEPTH, D, 2 * W], F32, "ExternalInput")
    w_up = dram("w_up", [DEPTH, 5, W, D], F32, "ExternalInput")
    w_out = dram("w_out", [DEPTH, D, D], F32, "ExternalInput")
    cst = dram("cst", [128, NCF], F32, "ExternalInput")
    cst_am = dram("cst_am", [128, 2048], F32, "ExternalInput")
    y_out = dram("y", [TT, D], F32, "ExternalOutput")
    wb = dram("wb", [DEPTH, D, DIN], BF16, "Internal")
    wsw = dram("wsw", [DEPTH, D, 1024], BF16, "Internal")
    wkvb = dram("wkvb", [DEPTH, D, 1024], BF16, "Internal")
    wupb = dram("wupb", [DEPTH, 5 * W, D], BF16, "Internal")
    woutb = dram("woutb", [DEPTH, D, D], BF16, "Internal")
    x1 = dram("x1", [TT, D], F32, "Internal")
    kTs = dram("kTs", [NH, 128, TT], BF16, "Internal")
    vS = dram("vS", [NH, 128, NTILE, 128], BF16, "Internal")
    d_wb, d_wsw, d_wkvb, d_wupb, d_woutb = T(wb), T(wsw), T(wkvb), T(wupb), T(woutb)
    d_x1, d_kTs, d_vS, d_y = T(x1), T(kTs), T(vS), T(y_out)

    import contextlib
    scopes = [contextlib.ExitStack()]
    uniq = [0]
    cur_bytes = [0]
    max_bytes = [0]

    def sb(name, shape, dt=F32):
        n = 1
        for s in shape[1:]:
            n *= s
        nb = n * (4 if dt in (F32, I32) else 2)
        cur_bytes[0] += nb
        max_bytes[0] = max(max_bytes[0], cur_bytes[0])
        uniq[0] += 1
        t = scopes[-1].enter_context(nc.sbuf_tensor("%s_%d" % (name, uniq[0]), list(shape), dt))
        scopes[-1].callback(lambda: cur_bytes.__setitem__(0, cur_bytes[0] - nb))
        return T(t, name)

    @contextlib.contextmanager
    def branch():
        scopes.append(contextlib.ExitStack())
        try:
            yield
        finally:
            S.barrier()
            rot.clear()
            scopes.pop().close()

    banks = [T(nc.psum_tensor("bank%d" % i, [128, 512], F32).__enter__(), "bank%d" % i) for i in range(8)]
    bank_rr = [0]
    gen_banks = [6, 7]

    def nbank():
        b = banks[gen_banks[bank_rr[0] % len(gen_banks)]]
        bank_rr[0] += 1
        return b

    def set_gen(lst):
        gen_banks[:] = list(lst)

    rot = {}

    def rtile(key, n, mk):
        if key not in rot:
            rot[key] = [[mk(i) for i in range(n)], 0]
        lst, i = rot[key]
        rot[key][1] = i + 1
        return lst[i % n]

    cf = sb("cf", [128, NCF])
    S.dma(cf[:], cst[:, :], writes=[cf])

    def cview(n):
        o, w_ = coffs[n]
        return cf[:, o:o + w_]

    def cbf(name, n, width):
        t = sb(name, [128, width], BF16)
        S.op("dve", lambda e: e.tensor_copy(out=t[:], in_=cview(n)), reads=[cf], writes=[t])
        return t

    identb = cbf("identb", "ident", 128)
    negUb = cbf("negUb", "negU", 128)
    negLb = cbf("negLb", "negL", 128)
    mlmaskb = cbf("mlmaskb", "mlmask", 128)
    amaskb = sb("amaskb", [128, 2048], BF16)
    onesb = sb("onesb", [128, 128], BF16)
    S.op("dve", lambda e: e.memset(onesb[:], 1.0), writes=[onesb])
    mhalf = sb("mhalf", [128, 1])
    S.op("dve", lambda e: e.memset(mhalf[:], -0.5), writes=[mhalf])
    identf = cview("ident")

    def selv(h):
        o, _ = coffs["sel"]
        return cf[0:4, o + h * 128:o + (h + 1) * 128]

    i4v = cf[0:4, coffs["i4"][0]:coffs["i4"][0] + 4]

    cast_rr = [0]

    def cast_copy(out_t, out_ap, in_t, in_ap):
        k = cast_rr[0] % 3
        cast_rr[0] += 1
        if k == 0:
            S.op("dve", lambda e: e.tensor_copy(out=out_ap, in_=in_ap), reads=[in_t], writes=[out_t])
        elif k == 1:
            S.op("act", lambda e: e.copy(out=out_ap, in_=in_ap), reads=[in_t], writes=[out_t])
        else:
            S.op("pool", lambda e: e.tensor_copy(out=out_ap, in_=in_ap), reads=[in_t], writes=[out_t])

    PW = 1024

    def prep_matrix(src2d, dst_t, dst2d, nrows, ncols):
        for r0 in range(0, nrows, 128):
            for c0 in range(0, ncols, PW):
                cw = min(PW, ncols - c0)
                f = rtile("prep_f", 3, lambda i: sb("prep_f%d" % i, [128, PW], F32))
                b = rtile("prep_b", 3, lambda i: sb("prep_b%d" % i, [128, PW], BF16))
                S.dma(f[:, 0:cw], src2d[r0:r0 + 128, c0:c0 + cw], writes=[f])
                cast_copy(b, b[:, 0:cw], f, f[:, 0:cw])
                S.dma(dst2d[r0:r0 + 128, c0:c0 + cw], b[:, 0:cw], reads=[b], writes=[dst_t])

    with branch():
        for hf in range(2):
            f = rtile("prep_f", 3, lambda i: sb("prep_f%d" % i, [128, PW], F32))
            S.dma(f[:, :], cst_am[:, hf * 1024:(hf + 1) * 1024], writes=[f])
            S.op("dve", lambda e: e.tensor_copy(out=amaskb[:, hf * 1024:(hf + 1) * 1024], in_=f[:, :]),
                 reads=[f], writes=[amaskb])
        for l in range(DEPTH):
            prep_matrix(w_in[l], d_wb, wb[l], D, DIN)
            prep_matrix(w_mem_kv[l], d_wkvb, wkvb[l], D, 1024)
            prep_matrix(w_up[l].rearrange("n w d -> (n w) d"), d_wupb, wupb[l], 5 * W, D)
            prep_matrix(w_out[l], d_woutb, woutb[l], D, D)
            for r0 in range(0, D, 128):
                f = rtile("prep_f", 3, None)
                b = rtile("prep_b", 3, None)
                S.dma(f[:, 0:1024], w_in[l][r0:r0 + 128, O_RTQ:O_RTQ + 1024], writes=[f])
                cast_copy(b, b[:, 0:1024], f, f[:, 0:1024])
                bv = b[:, 0:1024].rearrange("p (h two d) -> p h two d", two=2, d=64)
                dv = wsw[l][r0:r0 + 128, :].rearrange("p (h two d) -> p h two d", two=2, d=64)
                for s_ in range(2):
                    S.dma(dv[:, :, 1 - s_, :], bv[:, :, s_, :], reads=[b], writes=[d_wsw])

    hT = sb("hT", [128, 8, TG], BF16)
    ss4 = sb("ss4", [128, 4])
    rstd4 = sb("rstd4", [128, 4])
    gcol_pre = sb("gcol_pre", [128, 8])
    gcol_mem = sb("gcol_mem", [128, 8])
    gpost = sb("gpost", [128, D])
    lng = sb("lng", [128, W])
    lnb = sb("lnb", [128, W])
    cq = sb("cq", [128, NH, 4])
    ck = sb("ck", [128, NH, 4])
    gbs = sb("gbs", [128, 4])
    gwT = sb("gwT", [128, 4, 128], BF16)
    big = sb("big", [4, 1])
    bfg = sb("bfg", [4, 1])
    nbfg = sb("nbfg", [4, 1])
    memkT = sb("memkT", [128, NH, NMEM], BF16)
    memv = sb("memv", [128, 2, W], BF16)
    Wt = [sb("Wt%d" % i, [128, 8, 512], BF16) for i in range(3)]
    wt_rr = [0]
    ysT = [sb("ysT%d" % n, [128, NH, TG], BF16) for n in range(5)]
    Cst = sb("Cst", [128, NH, 129])
    Rst = sb("Rst", [128, NH, 128])
    mcar = sb("mcar", [4, 1])
    halo = sb("halo", [128, 2, NH, 3])
    st1 = sb("st1", [128, 4])
    st2 = sb("st2", [128, 4])
    st3 = sb("st3", [128, 4])
    den4 = sb("den4", [128, 4])

    def loadW(src_l, c0, ncols, src_t=None):
        t = Wt[wt_rr[0] % len(Wt)]
        wt_rr[0] += 1
        S.dma(t[:, :, 0:ncols], src_l.rearrange("(c p) n -> p c n", p=128)[:, :, c0:c0 + ncols],
              reads=[src_t if src_t is not None else d_wb], writes=[t])
        return t

    def rsqrt_cols(out_t, in_t, n, scale):
        S.op("dve", lambda e: e.tensor_scalar(out=in_t[:, 0:n], in0=in_t[:, 0:n], scalar1=scale, scalar2=EPS,
                                              op0=ALU.mult, op1=ALU.add), reads=[in_t], writes=[in_t])
        S.op("pool", lambda e: e.tensor_tensor(out=out_t[:, 0:n], in0=in_t[:, 0:n],
                                               in1=mhalf[:, 0:1].broadcast_to([128, n]), op=ALU.pow),
             reads=[in_t, mhalf], writes=[out_t])

    def norm_transpose(src_tiles, gcol, dstT, ntile, junk, xn):
        for i in range(ntile):
            S.op("act", lambda e: e.activation(out=junk[:], in_=src_tiles[i][:], func=AF.Square,
                                               accum_out=ss4[:, i:i + 1]),
                 reads=[src_tiles[i]], writes=[junk, ss4])
        rsqrt_cols(rstd4, ss4, ntile, 1.0 / D)
        for i in range(ntile):
            xb = xn[i % 2]
            S.op("dve", lambda e: e.tensor_scalar(out=xb[:], in0=src_tiles[i][:], scalar1=rstd4[:, i:i + 1],
                                                  scalar2=None, op0=ALU.mult),
                 reads=[src_tiles[i], rstd4], writes=[xb])
            ps = nbank()
            psb = ps[:].bitcast(BF16)
            for kc in range(8):
                S.op("pe", lambda e: e.transpose(out=psb[:, kc * 128:(kc + 1) * 128],
                                                 in_=xb[:, kc * 128:(kc + 1) * 128], identity=identb[:]),
                     reads=[xb, identb], writes=[ps])
            S.op("dve", lambda e: e.tensor_tensor(
                out=dstT[:, :, i * 128:(i + 1) * 128],
                in0=psb.rearrange("p (c t) -> p c t", t=128),
                in1=gcol[:, :].unsqueeze(2).broadcast_to([128, 8, 128]), op=ALU.mult),
                 reads=[ps, gcol], writes=[dstT])

    def mm_acc(ps, out_ap, pairs, extra_reads=(), start=True, stop=True):
        n = len(pairs)
        for i, (l_ap, r_ap) in enumerate(pairs):
            S.op("pe", lambda e: e.matmul(out_ap, lhsT=l_ap, rhs=r_ap, start=(start and i == 0),
                                          stop=(stop and i == n - 1)),
                 reads=list(extra_reads), writes=[ps])

    def proj_fm(wt, j, src_T, ps, out_ap, ntok=TG, extra=()):
        mm_acc(ps, out_ap, [(wt[:, kc, j * 128:(j + 1) * 128], src_T[:, kc, 0:ntok]) for kc in range(8)],
               extra_reads=[wt, src_T] + list(extra))

    def proj_tm(wt, i, src_T, ps, out_ap, ncols=512, extra=()):
        mm_acc(ps, out_ap, [(src_T[:, kc, i * 128:(i + 1) * 128], wt[:, kc, 0:ncols]) for kc in range(8)],
               extra_reads=[wt, src_T] + list(extra))

    def gelu_tanh(dst_t, dst_ap, ps, src_ap, tmpA, tmpB):
        S.op("act", lambda e: e.copy(out=tmpA[:], in_=src_ap), reads=[ps], writes=[tmpA])
        S.op("dve", lambda e: e.tensor_tensor(out=tmpB[:], in0=tmpA[:], in1=tmpA[:], op=ALU.mult),
             reads=[tmpA], writes=[tmpB])
        S.op("dve", lambda e: e.tensor_scalar(out=tmpB[:], in0=tmpB[:], scalar1=0.044715, scalar2=1.0,
                                              op0=ALU.mult, op1=ALU.add), reads=[tmpB], writes=[tmpB])
        S.op("dve", lambda e: e.tensor_tensor(out=tmpB[:], in0=tmpB[:], in1=tmpA[:], op=ALU.mult),
             reads=[tmpB, tmpA], writes=[tmpB])
        S.op("act", lambda e: e.activation(out=tmpB[:], in_=tmpB[:], func=AF.Sigmoid, scale=1.5957691216057308),
             reads=[tmpB], writes=[tmpB])
        S.op("dve", lambda e: e.tensor_tensor(out=dst_ap, in0=tmpB[:], in1=tmpA[:], op=ALU.mult),
             reads=[tmpB, tmpA], writes=[dst_t])

    def head_norm_tok(src_t, hsq):
        S.op("dve", lambda e: e.tensor_reduce(out=st1[:], in_=src_t[:], axis=AX.X, op=ALU.add),
             reads=[src_t], writes=[st1])
        S.op("pool", lambda e: e.tensor_tensor(out=hsq[:], in0=src_t[:], in1=src_t[:], op=ALU.mult),
             reads=[src_t], writes=[hsq])
        S.op("dve", lambda e: e.tensor_reduce(out=st2[:], in_=hsq[:], axis=AX.X, op=ALU.add),
             reads=[hsq], writes=[st2])
        S.op("dve", lambda e: e.tensor_scalar(out=st1[:], in0=st1[:], scalar1=1.0 / 128, scalar2=None,
                                              op0=ALU.mult), reads=[st1], writes=[st1])
        S.op("dve", lambda e: e.tensor_tensor(out=st3[:], in0=st1[:], in1=st1[:], op=ALU.mult),
             reads=[st1], writes=[st3])
        S.op("dve", lambda e: e.scalar_tensor_tensor(out=st2[:], in0=st2[:], scalar=1.0 / 128, in1=st3[:],
                                                     op0=ALU.mult, op1=ALU.subtract),
             reads=[st2, st3], writes=[st2])
        rsqrt_cols(st3, st2, 4, 1.0)
        S.op("dve", lambda e: e.tensor_tensor(out=src_t[:], in0=src_t[:],
                                              in1=st1[:, :].unsqueeze(2).broadcast_to([128, NH, 128]),
                                              op=ALU.subtract), reads=[src_t, st1], writes=[src_t])
        S.op("dve", lambda e: e.tensor_tensor(out=src_t[:], in0=src_t[:],
                                              in1=st3[:, :].unsqueeze(2).broadcast_to([128, NH, 128]),
                                              op=ALU.mult), reads=[src_t, st3], writes=[src_t])

    def tok_to_fm(src_b, dst_T, i):
        ps = nbank()
        psb = ps[:].bitcast(BF16)
        for h in range(NH):
            S.op("pe", lambda e: e.transpose(out=psb[:, h * 128:(h + 1) * 128], in_=src_b[:, h * 128:(h + 1) * 128],
                                             identity=identb[:]), reads=[src_b, identb], writes=[ps])
        S.op("act", lambda e: e.copy(out=dst_T[:, :, i * 128:(i + 1) * 128],
                                     in_=psb[:, 0:512].rearrange("p (h t) -> p h t", t=128)),
             reads=[ps], writes=[dst_T])

    for l in range(DEPTH):
        xsrc, xsrc_t = (x_in, None) if l == 0 else (x1, d_x1)
        xdst, xdst_t = (x1, d_x1) if l < DEPTH - 1 else (y_out, d_y)
        with branch():
            S.dma(gcol_pre[:], norm_pre[l].rearrange("(c p) -> p c", p=128), writes=[gcol_pre],
                  allow_slow_non_contiguous=True)
            S.dma(gcol_mem[:], mem_norm[l].rearrange("(c p) -> p c", p=128), writes=[gcol_mem],
                  allow_slow_non_contiguous=True)
            S.dma(gpost[:], norm_post[l:l + 1, :].broadcast_to([128, D]), writes=[gpost])
            S.dma(lng[:], gm_ln_g[l:l + 1, :].broadcast_to([128, W]), writes=[lng])
            S.dma(lnb[:], gm_ln_b[l:l + 1, :].broadcast_to([128, W]), writes=[lnb])
            for h in range(NH):
                S.dma(cq[:, h, :], conv_q[l][:, h * 128:(h + 1) * 128].rearrange("i d -> d i"), writes=[cq],
                      allow_slow_non_contiguous=True)
                S.dma(ck[:, h, :], conv_k[l][:, h * 128:(h + 1) * 128].rearrange("i d -> d i"), writes=[ck],
                      allow_slow_non_contiguous=True)
            S.dma(gbs[:], gm_bs[l].rearrange("g p -> p g"), writes=[gbs], allow_slow_non_contiguous=True)
            gw_raw = sb("gw_raw", [128, 4, 128])
            S.dma(gw_raw[:], gm_ws[l].rearrange("g p q -> p g q"), writes=[gw_raw])
            S.dma(big[:], b_ig[l].rearrange("(h o) -> h o", o=1), writes=[big])
            S.dma(bfg[:], b_fg[l].rearrange("(h o) -> h o", o=1), writes=[bfg])
            S.op("dve", lambda e: e.tensor_scalar(out=nbfg[:], in0=bfg[:], scalar1=-1.0, scalar2=None, op0=ALU.mult),
                 reads=[bfg], writes=[nbfg])
            for g_ in range(4):
                ps = nbank()
                S.op("pe", lambda e: e.transpose(out=ps[:, 0:128], in_=gw_raw[:, g_, :], identity=identf),
                     reads=[gw_raw, cf], writes=[ps])
                S.op("dve", lambda e: e.tensor_tensor(out=gwT[:, g_, :], in0=ps[:, 0:128], in1=cview("gmmask"),
                                                      op=ALU.mult), reads=[ps, cf], writes=[gwT])
            mt_ = [sb("memx%d" % i, [128, D]) for i in range(2)]
            junk = sb("junk", [128, D])
            xn = [sb("xn%d" % i, [128, D], BF16) for i in range(2)]
            memT = sb("memT", [128, 8, NMEM], BF16)
            for i in range(2):
                S.dma(mt_[i][:], mem_in[i * 128:(i + 1) * 128, :], writes=[mt_[i]])
            norm_transpose(mt_, gcol_mem, memT, 2, junk, xn)
            wkv = loadW(wkvb[l], 0, 512, d_wkvb)
            for h in range(NH):
                ps = nbank()
                proj_fm(wkv, h, memT, ps, ps[:, 0:NMEM], ntok=NMEM)
                S.op("act", lambda e: e.copy(out=memkT[:, h, :], in_=ps[:, 0:NMEM]), reads=[ps], writes=[memkT])
            wkv = loadW(wkvb[l], 512, 512, d_wkvb)
            for i in range(2):
                ps = nbank()
                proj_tm(wkv, i, memT, ps, ps[:, :])
                S.op("act", lambda e: e.copy(out=memv[:, i, :], in_=ps[:, :]), reads=[ps], writes=[memv])
            S.op("dve", lambda e: e.memset(Cst[:], 0.0), writes=[Cst])
            S.op("dve", lambda e: e.memset(Rst[:], 0.0), writes=[Rst])
            S.op("dve", lambda e: e.memset(mcar[:], 0.0), writes=[mcar])
            S.op("dve", lambda e: e.memset(halo[:], 0.0), writes=[halo])

        for g in range(NG):
            t0 = g * TG
            with branch():
                set_gen(range(8))
                xt = [sb("xt%d" % i, [128, D]) for i in range(4)]
                junk = sb("junk", [128, D])
                xn = [sb("xn%d" % i, [128, D], BF16) for i in range(2)]
                for i in range(4):
                    S.dma(xt[i][:], xsrc[t0 + i * 128:t0 + (i + 1) * 128, :],
                          reads=[xsrc_t] if xsrc_t is not None else [], writes=[xt[i]])
                norm_transpose(xt, gcol_pre, hT, 4, junk, xn)

            with branch():
                set_gen([6, 7, 0, 1])
                qT = sb("qT", [128, NH, TG], BF16)
                kT = sb("kT", [128, NH, TG], BF16)
                vtok = sb("vtok", [128, 4, NH, 128], BF16)
                szT = sb("szT", [128, NH, TG])
                kpc = [sb("kpc%d" % i, [128, 1024], BF16) for i in range(4)]
                vpc = [sb("vpc%d" % i, [128, 8, 128], BF16) for i in range(4)]
                e_t = [sb("e_t%d" % i, [128, TG]) for i in range(6)]
                sp_t = [sb("sp_t%d" % i, [128, TG], BF16) for i in range(8)]
                g_t = [sb("g_t%d" % i, [128, TG]) for i in range(3)]
                w_t = [sb("w_t%d" % i, [128, TG], BF16) for i in range(4)]
                wt = loadW(wb[l], O_SBQ, 512)
                for h in range(NH):
                    ps = nbank()
                    proj_fm(wt, h, hT, ps, ps[:, :])
                    S.op("act", lambda e: e.activation(out=qT[:, h, :], in_=ps[:, :], func=AF.Copy, scale=SCALE),
                         reads=[ps], writes=[qT])
                wt = loadW(wb[l], O_SBK, 512)
                for h in range(NH):
                    ps = nbank()
                    proj_fm(wt, h, hT, ps, ps[:, :])
                    S.op("dve", lambda e: e.tensor_copy(out=kT[:, h, :], in_=ps[:, :]), reads=[ps], writes=[kT])
                S.dma(kTs[:, :, t0:t0 + TG].rearrange("h p t -> p h t"), kT[:], reads=[kT], writes=[d_kTs])
                wt = loadW(wb[l], O_SBV, 512)
                for i in range(4):
                    ps = nbank()
                    proj_tm(wt, i, hT, ps, ps[:, :])
                    S.op("act", lambda e: e.copy(out=vtok[:, i, :, :],
                                                 in_=ps[:, :].rearrange("p (h d) -> p h d", d=128)),
                         reads=[ps], writes=[vtok])
                for h in range(NH):
                    S.dma(vS[h, :, g * 4:(g + 1) * 4, :], vtok[:, :, h, :], reads=[vtok], writes=[d_vS])
                wt = loadW(wb[l], O_SBZ, 512)
                for h in range(NH):
                    ps = nbank()
                    proj_fm(wt, h, hT, ps, ps[:, :])
                    S.op("act", lambda e: e.activation(out=szT[:, h, :], in_=ps[:, :], func=AF.Silu),
                         reads=[ps], writes=[szT])
                nblk = 4 * g + 4
                PB = 8
                zb = [banks[0], banks[1], banks[6], banks[7]]
                for hp in range(2):
                    heads = (2 * hp, 2 * hp + 1)
                    units = [(j, hi) for j in range(nblk - 1, -1, -1) for hi in range(2)]
                    U = len(units)
                    loaded = {}
                    pbuf_rr = {0: 0, 1: 0}
                    st = {}

                    def ensure(hi, p):
                        if (hi, p) in loaded:
                            return
                        k_ = pbuf_rr[hi] % 2
                        pbuf_rr[hi] += 1
                        kt_ = kpc[hi * 2 + k_]
                        vt_ = vpc[hi * 2 + k_]
                        nb_ = min(PB, nblk - p * PB)
                        h = heads[hi]
                        S.dma(kt_[:, 0:nb_ * 128], kTs[h, :, p * PB * 128:p * PB * 128 + nb_ * 128],
                              reads=[d_kTs], writes=[kt_])
                        S.dma(vt_[:, 0:nb_, :], vS[h, :, p * PB:p * PB + nb_, :], reads=[d_vS], writes=[vt_])
                        loaded[(hi, p)] = (kt_, vt_)

                    def stA(u):
                        j, hi = units[u]
                        h = heads[hi]
                        p = j // PB
                        jj = j - p * PB
                        ensure(hi, p)
                        if p > 0 and (jj == PB // 2 or (jj < PB // 2 and j == nblk - 1)):
                            ensure(hi, p - 1)
                        kt_, vt_ = loaded[(hi, p)]
                        Z = zb[u % 4]
                        a = j - 4 * g
                        S.op("pe", lambda e: e.matmul(Z[:, :], lhsT=kt_[:, jj * 128:(jj + 1) * 128],
                                                      rhs=qT[:, h, :], start=True, stop=(a < 0)),
                             reads=[kt_, qT], writes=[Z])
                        if a >= 0:
                            S.op("pe", lambda e: e.matmul(Z[:, :], lhsT=identb[:],
                                                          rhs=amaskb[:, a * 512:(a + 1) * 512],
                                                          start=False, stop=True), reads=[identb, amaskb],
                                 writes=[Z])
                        st[u] = {"Z": Z, "vt": vt_, "jj": jj, "first": (j == nblk - 1), "last": (j == 0),
                                 "hi": hi}

                    def stB(u):
                        s_ = st[u]
                        et = e_t[u % len(e_t)]
                        spt = sp_t[u % len(sp_t)]
                        S.op("act", lambda e: e.activation(out=et[:], in_=s_["Z"][:, :], func=AF.Exp),
                             reads=[s_["Z"]], writes=[et])
                        S.op("act", lambda e: e.activation(out=spt[:], in_=et[:], func=AF.Ln, bias=1.0),
                             reads=[et], writes=[spt])
                        s_["et"] = et
                        s_["sp"] = spt

                    def stC(u):
                        s_ = st[u]
                        P = banks[2 + s_["hi"]]
                        if not s_["first"]:
                            psp = st[u - 2]["sp"]
                            S.op("pe", lambda e: e.matmul(P[:, :], lhsT=negLb[:], rhs=psp[:], start=False,
                                                          stop=False, skip_group_check=True),
                                 reads=[negLb, psp], writes=[P])
                        S.op("pe", lambda e: e.matmul(P[:, :], lhsT=negUb[:], rhs=s_["sp"][:], start=s_["first"],
                                                      stop=True, skip_group_check=True),
                             reads=[negUb, s_["sp"]], writes=[P])

                    def stD(u):
                        s_ = st[u]
                        P = banks[2 + s_["hi"]]
                        gt = g_t[u % len(g_t)]
                        S.op("act", lambda e: e.activation(out=gt[:], in_=P[:, :], func=AF.Exp),
                             reads=[P], writes=[gt])
                        wtl = w_t[u % len(w_t)]
                        eng_ = "dve" if (u % 2 == 0) else "pool"
                        S.op(eng_, lambda e: e.tensor_tensor(out=wtl[:], in0=s_["et"][:], in1=gt[:], op=ALU.mult),
                             reads=[s_["et"], gt], writes=[wtl])
                        s_["w"] = wtl

                    def stE(u):
                        s_ = st[u]
                        O = banks[4 + s_["hi"]]
                        S.op("pe", lambda e: e.matmul(O[:, :], lhsT=s_["vt"][:, s_["jj"], :], rhs=s_["w"][:],
                                                      start=s_["first"], stop=s_["last"], skip_group_check=True),
                             reads=[s_["vt"], s_["w"]], writes=[O])
                        if u >= 4:
                            st.pop(u - 4, None)

                    for t in range(U + 4):
                        if t < U:
                            stA(t)
                        if 0 <= t - 1 < U:
                            stB(t - 1)
                        if 0 <= t - 2 < U:
                            stC(t - 2)
                        if 0 <= t - 3 < U:
                            stD(t - 3)
                        if 0 <= t - 4 < U:
                            stE(t - 4)
                    for hi, h in enumerate(heads):
                        O = banks[4 + hi]
                        S.op("dve", lambda e: e.tensor_tensor(out=ysT[0][:, h, :], in0=O[:, :], in1=szT[:, h, :],
                                                              op=ALU.mult), reads=[O, szT], writes=[ysT[0]])

            with branch():
                set_gen([2, 3, 4, 5, 6, 7])
                qT = sb("qT", [128, NH, TG], BF16)
                szT = sb("szT", [128, NH, TG])
                w_t = [sb("w_t%d" % i, [128, TG], BF16) for i in range(4)]
                tmpA = sb("tmpA", [128, TG])
                wt = loadW(wb[l], O_XAQ, 512)
                for h in range(NH):
                    ps = nbank()
                    proj_fm(wt, h, hT, ps, ps[:, :])
                    S.op("act", lambda e: e.activation(out=qT[:, h, :], in_=ps[:, :], func=AF.Copy, scale=SCALE),
                         reads=[ps], writes=[qT])
                wt = loadW(wb[l], O_XAZ, 512)
                for h in range(NH):
                    ps = nbank()
                    proj_fm(wt, h, hT, ps, ps[:, :])
                    S.op("act", lambda e: e.activation(out=szT[:, h, :], in_=ps[:, :], func=AF.Silu),
                         reads=[ps], writes=[szT])
                for h in range(NH):
                    exs = []
                    for mc in range(2):
                        ps = nbank()
                        S.op("pe", lambda e: e.matmul(ps[:, :], lhsT=memkT[:, h, mc * 128:(mc + 1) * 128],
                                                      rhs=qT[:, h, :], start=True, stop=True),
                             reads=[memkT, qT], writes=[ps])
                        ex = rtile("w_t", 4, lambda i: w_t[i])
                        S.op("act", lambda e: e.activation(out=ex[:], in_=ps[:, :], func=AF.Exp),
                             reads=[ps], writes=[ex])
                        exs.append(ex)
                    pn = banks[0]
                    pd = banks[1]
                    for mc in range(2):
                        S.op("pe", lambda e: e.matmul(pn[:, :], lhsT=memv[:, mc, h * 128:(h + 1) * 128],
                                                      rhs=exs[mc][:], start=(mc == 0), stop=(mc == 1)),
                             reads=[memv, exs[mc]], writes=[pn])
                    for mc in range(2):
                        S.op("pe", lambda e: e.matmul(pd[:, :], lhsT=onesb[:], rhs=exs[mc][:],
                                                      start=(mc == 0), stop=(mc == 1)),
                             reads=[onesb, exs[mc]], writes=[pd])
                    S.op("dve", lambda e: e.reciprocal(out=tmpA[:], in_=pd[:, :]), reads=[pd], writes=[tmpA])
                    S.op("dve", lambda e: e.tensor_tensor(out=tmpA[:], in0=tmpA[:], in1=szT[:, h, :], op=ALU.mult),
                         reads=[tmpA, szT], writes=[tmpA])
                    S.op("dve", lambda e: e.tensor_tensor(out=ysT[4][:, h, :], in0=pn[:, :], in1=tmpA[:],
                                                          op=ALU.mult), reads=[pn, tmpA], writes=[ysT[4]])

            with branch():
                set_gen([2, 3, 4, 5, 6, 7])
                posi = sb("posi", [128, TG], I32)
                ang = sb("ang", [128, TG])
                kk = sb("kk", [128, TG])
                rr = sb("rr", [128, TG])
                cosT = sb("cosT", [128, TG])
                sinT = sb("sinT", [128, TG])
                tmpA = sb("tmpA", [128, TG])
                tmpB = sb("tmpB", [128, TG])
                qT = sb("qT", [128, NH, TG], BF16)
                kT = sb("kT", [128, NH, TG], BF16)
                qdT = sb("qdT", [128, NH, TG], BF16)
                vtok = sb("vtok", [128, 4, NH, 128], BF16)
                zsil = sb("zsil", [128, 4, W])
                kw = sb("kw", [128, 4, NH, 128], BF16)
                Rb = sb("Rb", [128, NH, 8, 128], BF16)
                scT = [sb("scT%d" % i, [128, 128], BF16) for i in range(2)]
                hm = sb("hm", [128, NH, 128])
                hsq = sb("hsq", [128, NH, 128])
                ytokb = sb("ytokb", [128, W], BF16)
                S.dma(posi[:], pos_in[0:1, t0:t0 + TG].broadcast_to([128, TG]), writes=[posi])
                S.op("dve", lambda e: e.tensor_copy(out=ang[:], in_=posi[:]), reads=[posi], writes=[ang])
                S.op("dve", lambda e: e.tensor_scalar(out=ang[:], in0=ang[:], scalar1=cview("invf"), scalar2=None,
                                                      op0=ALU.mult), reads=[ang, cf], writes=[ang])
                MAGIC = 12582912.0
                S.op("dve", lambda e: e.tensor_scalar(out=kk[:], in0=ang[:], scalar1=float(1.0 / (2 * np.pi)),
                                                      scalar2=MAGIC, op0=ALU.mult, op1=ALU.add),
                     reads=[ang], writes=[kk])
                S.op("dve", lambda e: e.tensor_scalar(out=kk[:], in0=kk[:], scalar1=MAGIC, scalar2=None,
                                                      op0=ALU.subtract), reads=[kk], writes=[kk])
                C1 = 6.28125
                C2 = float(np.float32(2 * np.pi - 6.28125))
                C3 = float(2 * np.pi - 6.28125 - np.float64(np.float32(2 * np.pi - 6.28125)))
                S.op("dve", lambda e: e.scalar_tensor_tensor(out=rr[:], in0=kk[:], scalar=-C1, in1=ang[:],
                                                             op0=ALU.mult, op1=ALU.add), reads=[kk, ang], writes=[rr])
                S.op("dve", lambda e: e.scalar_tensor_tensor(out=rr[:], in0=kk[:], scalar=-C2, in1=rr[:],
                                                             op0=ALU.mult, op1=ALU.add), reads=[kk, rr], writes=[rr])
                S.op("dve", lambda e: e.scalar_tensor_tensor(out=rr[:], in0=kk[:], scalar=-C3, in1=rr[:],
                                                             op0=ALU.mult, op1=ALU.add), reads=[kk, rr], writes=[rr])
                PI_IN = 3.1415925
                S.op("dve", lambda e: e.tensor_scalar(out=rr[:], in0=rr[:], scalar1=PI_IN, scalar2=-PI_IN,
                                                      op0=ALU.min, op1=ALU.max), reads=[rr], writes=[rr])
                S.op("dve", lambda e: e.tensor_scalar(out=kk[:], in0=rr[:], scalar1=-1.0, scalar2=None,
                                                      op0=ALU.mult), reads=[rr], writes=[kk])
                S.op("dve", lambda e: e.tensor_tensor(out=kk[:], in0=kk[:], in1=rr[:], op=ALU.min),
                     reads=[kk, rr], writes=[kk])
                S.op("dve", lambda e: e.tensor_scalar(out=kk[:], in0=kk[:], scalar1=float(np.pi / 2), scalar2=None,
                                                      op0=ALU.add), reads=[kk], writes=[kk])
                S.op("act", lambda e: e.activation(out=cosT[:], in_=kk[:], func=AF.Sin), reads=[kk], writes=[cosT])
                S.op("act", lambda e: e.activation(out=sinT[:], in_=rr[:], func=AF.Sin), reads=[rr], writes=[sinT])
                S.op("dve", lambda e: e.tensor_scalar(out=sinT[:], in0=sinT[:], scalar1=cview("sgn"), scalar2=None,
                                                      op0=ALU.mult), reads=[sinT, cf], writes=[sinT])

                def rot_proj(c0, sw0, dstT, scl):
                    wa = loadW(wb[l], c0, 512)
                    wsb = loadW(wsw[l], sw0, 512, d_wsw)
                    for h in range(NH):
                        p1 = nbank()
                        proj_fm(wa, h, hT, p1, p1[:, :])
                        S.op("dve", lambda e: e.scalar_tensor_tensor(out=tmpA[:], in0=p1[:, :], scalar=scl,
                                                                     in1=cosT[:], op0=ALU.mult, op1=ALU.mult),
                             reads=[p1, cosT], writes=[tmpA])
                        p2 = nbank()
                        proj_fm(wsb, h, hT, p2, p2[:, :])
                        S.op("dve", lambda e: e.scalar_tensor_tensor(out=tmpB[:], in0=p2[:, :], scalar=scl,
                                                                     in1=sinT[:], op0=ALU.mult, op1=ALU.mult),
                             reads=[p2, sinT], writes=[tmpB])
                        S.op("dve", lambda e: e.tensor_tensor(out=dstT[:, h, :], in0=tmpA[:], in1=tmpB[:],
                                                              op=ALU.add), reads=[tmpA, tmpB], writes=[dstT])

                rot_proj(O_RTQ, 0, qT, 1.0)
                rot_proj(O_RTK, 512, kT, SCALE)
                rwq_o = coffs["rwq"][0]
                for h in range(NH):
                    S.op("dve", lambda e: e.tensor_tensor(
                        out=qdT[:, h, :].rearrange("p (c l) -> p c l", l=64),
                        in0=qT[:, h, :].rearrange("p (c l) -> p c l", l=64),
                        in1=cf[:, rwq_o + h * 64:rwq_o + (h + 1) * 64].unsqueeze(1).broadcast_to([128, 8, 64]),
                        op=ALU.mult), reads=[qT, cf], writes=[qdT])
                wt = loadW(wb[l], O_RTV, 512)
                for i in range(4):
                    ps = nbank()
                    proj_tm(wt, i, hT, ps, ps[:, :])
                    S.op("act", lambda e: e.copy(out=vtok[:, i, :, :],
                                                 in_=ps[:, :].rearrange("p (h d) -> p h d", d=128)),
                         reads=[ps], writes=[vtok])
                wt = loadW(wb[l], O_RTZ, 512)
                for i in range(4):
                    ps = nbank()
                    proj_tm(wt, i, hT, ps, ps[:, :])
                    S.op("act", lambda e: e.activation(out=zsil[:, i, :], in_=ps[:, :], func=AF.Silu),
                         reads=[ps], writes=[zsil])
                rwe_o = coffs["rwend"][0]
                for i in range(4):
                    ps = nbank()
                    psb = ps[:].bitcast(BF16)
                    for h in range(NH):
                        S.op("pe", lambda e: e.transpose(out=psb[:, h * 128:(h + 1) * 128],
                                                         in_=kT[:, h, i * 128:(i + 1) * 128], identity=identb[:]),
                             reads=[kT, identb], writes=[ps])
                    S.op("dve", lambda e: e.tensor_tensor(
                        out=kw[:, i, :, :], in0=psb[:, 0:512].rearrange("p (h d) -> p h d", d=128),
                        in1=cf[:, rwe_o:rwe_o + 4].unsqueeze(2).broadcast_to([128, NH, 128]), op=ALU.mult),
                         reads=[ps, cf], writes=[kw])
                for c in range(8):
                    i, half_ = c // 2, (c % 2) * 64
                    for h in range(NH):
                        S.op("act", lambda e: e.copy(out=Rb[:, h, c, :], in_=Rst[:, h, :]), reads=[Rst], writes=[Rb])
                        ps = nbank()
                        S.op("pe", lambda e: e.matmul(ps[:, 0:128], lhsT=kw[half_:half_ + 64, i, h, :],
                                                      rhs=vtok[half_:half_ + 64, i, h, :], start=True, stop=True),
                             reads=[kw, vtok], writes=[ps])
                        S.op("dve", lambda e: e.scalar_tensor_tensor(out=Rst[:, h, :], in0=Rst[:, h, :],
                                                                     scalar=float(rcd[h]), in1=ps[:, 0:128],
                                                                     op0=ALU.mult, op1=ALU.add),
                             reads=[Rst, ps], writes=[Rst])
                rd_o = coffs["rdecay"][0]
                for i in range(4):
                    po = banks[0 + (i % 2)]
                    for h in range(NH):
                        ps = nbank()
                        S.op("pe", lambda e: e.matmul(ps[:, 0:128], lhsT=kT[:, h, i * 128:(i + 1) * 128],
                                                      rhs=qT[:, h, i * 128:(i + 1) * 128], start=True, stop=True),
                             reads=[kT, qT], writes=[ps])
                        sc = scT[h % 2]
                        S.op("dve", lambda e: e.tensor_tensor(out=sc[:], in0=ps[:, 0:128],
                                                              in1=cf[:, rd_o + h * 128:rd_o + (h + 1) * 128],
                                                              op=ALU.mult), reads=[ps, cf], writes=[sc])
                        S.op("pe", lambda e: e.matmul(po[:, h * 128:(h + 1) * 128], lhsT=sc[:], rhs=vtok[:, i, h, :],
                                                      start=True, stop=False, skip_group_check=True),
                             reads=[sc, vtok], writes=[po])
                        for hf in range(2):
                            c = 2 * i + hf
                            S.op("pe", lambda e: e.matmul(po[hf * 64:(hf + 1) * 64, h * 128:(h + 1) * 128],
                                                          lhsT=qdT[:, h, i * 128 + hf * 64:i * 128 + (hf + 1) * 64],
                                                          rhs=Rb[:, h, c, :], start=False, stop=(hf == 1),
                                                          skip_group_check=True),
                                 reads=[qdT, Rb], writes=[po])
                    S.op("act", lambda e: e.copy(out=hm[:], in_=po[:, :].rearrange("p (h d) -> p h d", d=128)),
                         reads=[po], writes=[hm])
                    head_norm_tok(hm, hsq)
                    S.op("dve", lambda e: e.tensor_tensor(out=ytokb[:], in0=hm[:].rearrange("p h d -> p (h d)"),
                                                          in1=zsil[:, i, :], op=ALU.mult), reads=[hm, zsil],
                         writes=[ytokb])
                    tok_to_fm(ytokb, ysT[3], i)

            with branch():
                set_gen(range(8))
                gu = sb("gu", [128, 4, W])
                gv = sb("gv", [128, 4, W])
                zsil = sb("zsil", [128, 4, W])
                gvb = sb("gvb", [128, W], BF16)
                ytok = sb("ytok", [128, W])
                ytokb = sb("ytokb", [128, W], BF16)
                tmpA = sb("tmpA", [128, TG])
                tmpB = sb("tmpB", [128, TG])
                junk = sb("junk", [128, W])
                wt = loadW(wb[l], O_GMU, 512)
                for i in range(4):
                    ps = nbank()
                    proj_tm(wt, i, hT, ps, ps[:, :])
                    gelu_tanh(gu, gu[:, i, :], ps, ps[:, :], tmpA, tmpB)
                wt = loadW(wb[l], O_GMV, 512)
                for i in range(4):
                    ps = nbank()
                    proj_tm(wt, i, hT, ps, ps[:, :])
                    gelu_tanh(gv, gv[:, i, :], ps, ps[:, :], tmpA, tmpB)
                wt = loadW(wb[l], O_GMZ, 512)
                for i in range(4):
                    ps = nbank()
                    proj_tm(wt, i, hT, ps, ps[:, :])
                    S.op("act", lambda e: e.activation(out=zsil[:, i, :], in_=ps[:, :], func=AF.Silu),
                         reads=[ps], writes=[zsil])
                for i in range(4):
                    S.op("dve", lambda e: e.tensor_reduce(out=st1[:, 0:1], in_=gv[:, i, :], axis=AX.X, op=ALU.add),
                         reads=[gv], writes=[st1])
                    S.op("act", lambda e: e.activation(out=junk[:], in_=gv[:, i, :], func=AF.Square,
                                                       accum_out=st2[:, 0:1]), reads=[gv], writes=[junk, st2])
                    S.op("dve", lambda e: e.tensor_scalar(out=st1[:, 0:1], in0=st1[:, 0:1], scalar1=1.0 / W,
                                                          scalar2=None, op0=ALU.mult), reads=[st1], writes=[st1])
                    S.op("dve", lambda e: e.tensor_tensor(out=st3[:, 0:1], in0=st1[:, 0:1], in1=st1[:, 0:1],
                                                          op=ALU.mult), reads=[st1], writes=[st3])
                    S.op("dve", lambda e: e.scalar_tensor_tensor(out=st2[:, 0:1], in0=st2[:, 0:1], scalar=1.0 / W,
                                                                 in1=st3[:, 0:1], op0=ALU.mult, op1=ALU.subtract),
                         reads=[st2, st3], writes=[st2])
                    rsqrt_cols(st3, st2, 1, 1.0)
                    S.op("dve", lambda e: e.tensor_scalar(out=ytok[:], in0=gv[:, i, :], scalar1=st1[:, 0:1],
                                                          scalar2=st3[:, 0:1], op0=ALU.subtract, op1=ALU.mult),
                         reads=[gv, st1, st3], writes=[ytok])
                    S.op("dve", lambda e: e.tensor_tensor(out=ytok[:], in0=ytok[:], in1=lng[:], op=ALU.mult),
                         reads=[ytok, lng], writes=[ytok])
                    S.op("dve", lambda e: e.tensor_tensor(out=gvb[:], in0=ytok[:], in1=lnb[:], op=ALU.add),
                         reads=[ytok, lnb], writes=[gvb])
                    ps = nbank()
                    for g_ in range(4):
                        S.op("pe", lambda e: e.matmul(ps[:, g_ * 128:(g_ + 1) * 128], lhsT=gwT[:, g_, :],
                                                      rhs=gvb[:, g_ * 128:(g_ + 1) * 128], start=True, stop=True,
                                                      skip_group_check=True),
                             reads=[gwT, gvb], writes=[ps])
                    for g_ in range(4):
                        S.op("dve", lambda e: e.scalar_tensor_tensor(
                            out=ytok[:, g_ * 128:(g_ + 1) * 128], in0=ps[:, g_ * 128:(g_ + 1) * 128],
                            scalar=gbs[:, g_:g_ + 1], in1=gu[:, i, g_ * 128:(g_ + 1) * 128], op0=ALU.add,
                            op1=ALU.mult), reads=[ps, gbs, gu], writes=[ytok])
                    S.op("dve", lambda e: e.tensor_tensor(out=ytokb[:], in0=ytok[:], in1=zsil[:, i, :], op=ALU.mult),
                         reads=[ytok, zsil], writes=[ytokb])
                    tok_to_fm(ytokb, ysT[2], i)

            with branch():
                set_gen([6, 7])
                pre = sb("pre", [128, NH, TG + 3])
                cva = sb("cva", [128, NH, TG])
                qT = sb("qT", [128, NH, TG], BF16)
                kT = sb("kT", [128, NH, TG], BF16)
                qdT = sb("qdT", [128, NH, TG], BF16)
                kw = sb("kw", [128, 4, NH, 128], BF16)
                vaug = sb("vaug", [128, 4, NH, 129], BF16)
                oz = sb("oz", [128, 4, W])
                Cb = sb("Cb", [128, NH, 8, 129], BF16)
                Ctmp = sb("Ctmp", [128, 129])
                R = {n: sb("row_" + n, [4, TG]) for n in
                     ["i", "sp", "bcum", "a", "g", "bm", "mt", "dint", "r1", "emt", "wend"]}
                mseq = sb("mseq", [4, 8])
                mprev = sb("mprev", [4, 8])
                dlt = sb("dlt", [4, 8])
                dold_loc = sb("dold_loc", [4, 16])
                colsb = sb("colsb", [128, 4, 2, 4])
                dcol = sb("dcol", [128, NH, 16])
                Dm = [sb("Dm%d" % i, [128, 128]) for i in range(2)]
                scT = [sb("scT%d" % i, [128, 128], BF16) for i in range(2)]
                hm = sb("hm", [128, NH, 128])
                hsq = sb("hsq", [128, NH, 128])
                ytok = sb("ytok", [128, W])
                ytokb = sb("ytokb", [128, W], BF16)

                def conv_branch(c0, hidx, cw, dstT, scl):
                    wt_ = loadW(wb[l], c0, 512)
                    S.op("pool", lambda e: e.tensor_copy(out=pre[:, :, 0:3], in_=halo[:, hidx, :, :]),
                         reads=[halo], writes=[pre])
                    for h in range(NH):
                        ps = nbank()
                        proj_fm(wt_, h, hT, ps, ps[:, :])
                        S.op("act", lambda e: e.copy(out=pre[:, h, 3:TG + 3], in_=ps[:, :]), reads=[ps], writes=[pre])
                    S.op("pool", lambda e: e.tensor_copy(out=halo[:, hidx, :, :], in_=pre[:, :, TG:TG + 3]),
                         reads=[pre], writes=[halo])
                    for h in range(NH):
                        S.op("dve", lambda e: e.tensor_scalar(out=cva[:, h, :], in0=pre[:, h, 0:TG],
                                                              scalar1=cw[:, h, 0:1], scalar2=None, op0=ALU.mult),
                             reads=[pre, cw], writes=[cva])
                        for tap in range(1, 4):
                            S.op("dve", lambda e: e.scalar_tensor_tensor(
                                out=cva[:, h, :], in0=pre[:, h, tap:tap + TG], scalar=cw[:, h, tap:tap + 1],
                                in1=cva[:, h, :], op0=ALU.mult, op1=ALU.add), reads=[pre, cw, cva], writes=[cva])
                    S.op("act", lambda e: e.activation(out=cva[:], in_=cva[:], func=AF.Silu), reads=[cva], writes=[cva])
                    S.op("dve", lambda e: e.tensor_scalar(out=dstT[:], in0=cva[:], scalar1=scl, scalar2=None,
                                                          op0=ALU.mult), reads=[cva], writes=[dstT])

                conv_branch(O_MLQ, 0, cq, qT, 1.0)
                conv_branch(O_MLK, 1, ck, kT, SCALE)
                wt = loadW(wb[l], O_MLV, 512)
                for i in range(4):
                    ps = nbank()
                    proj_tm(wt, i, hT, ps, ps[:, :])
                    S.op("act", lambda e: e.copy(out=vaug[:, i, :, 0:128],
                                                 in_=ps[:, :].rearrange("p (h d) -> p h d", d=128)),
                         reads=[ps], writes=[vaug])
                S.op("dve", lambda e: e.memset(vaug[:, :, :, 128:129], 1.0), writes=[vaug])
                wt = loadW(wb[l], O_MLO, 512)
                for i in range(4):
                    ps = nbank()
                    proj_tm(wt, i, hT, ps, ps[:, :])
                    S.op("act", lambda e: e.activation(out=oz[:, i, :], in_=ps[:, :], func=AF.Sigmoid),
                         reads=[ps], writes=[oz])
                wt = loadW(wb[l], O_MLZ, 512)
                for i in range(4):
                    ps = nbank()
                    proj_tm(wt, i, hT, ps, ps[:, :])
                    S.op("act", lambda e: e.activation(out=ytok[:], in_=ps[:, :], func=AF.Silu),
                         reads=[ps], writes=[ytok])
                    S.op("dve", lambda e: e.tensor_tensor(out=oz[:, i, :], in0=oz[:, i, :], in1=ytok[:], op=ALU.mult),
                         reads=[oz, ytok], writes=[oz])
                wt = loadW(wb[l], O_MLI, 8)
                ps = nbank()
                mm_acc(ps, ps[0:4, :], [(wt[:, kc, 0:4], hT[:, kc, :]) for kc in range(8)], extra_reads=[wt, hT])
                S.op("act", lambda e: e.activation(out=R["i"][:], in_=ps[0:4, :], func=AF.Identity, bias=big[:, 0:1]),
                     reads=[ps, big], writes=[R["i"]])
                ps = nbank()
                mm_acc(ps, ps[0:4, :], [(wt[:, kc, 4:8], hT[:, kc, :]) for kc in range(8)], extra_reads=[wt, hT])
                S.op("act", lambda e: e.activation(out=R["sp"][:], in_=ps[0:4, :], func=AF.Exp, scale=-1.0,
                                                   bias=nbfg[:, 0:1]), reads=[ps, nbfg], writes=[R["sp"]])
                S.op("act", lambda e: e.activation(out=R["sp"][:], in_=R["sp"][:], func=AF.Ln, bias=1.0),
                     reads=[R["sp"]], writes=[R["sp"]])
                rm_o = coffs["resetm"][0]
                nr_o = coffs["negreset"][0]
                S.op("dve", lambda e: e.tensor_tensor_scan(out=R["bcum"][:], data0=cf[0:4, rm_o:rm_o + TG],
                                                           data1=R["sp"][:], initial=0.0, op0=ALU.mult,
                                                           op1=ALU.subtract), reads=[cf, R["sp"]], writes=[R["bcum"]])
                S.op("dve", lambda e: e.tensor_tensor(out=R["a"][:], in0=R["i"][:], in1=R["bcum"][:],
                                                      op=ALU.subtract), reads=[R["i"], R["bcum"]], writes=[R["a"]])
                S.op("dve", lambda e: e.tensor_tensor_scan(out=R["g"][:], data0=cf[0:4, nr_o:nr_o + TG],
                                                           data1=R["a"][:], initial=-1e30, op0=ALU.add, op1=ALU.max),
                     reads=[cf, R["a"]], writes=[R["g"]])
                S.op("dve", lambda e: e.tensor_tensor(out=R["g"][:], in0=R["bcum"][:], in1=R["g"][:], op=ALU.add),
                     reads=[R["bcum"], R["g"]], writes=[R["g"]])
                blast = R["bcum"][:].rearrange("p (c l) -> p c l", l=64)[:, :, 63]
                alast = R["g"][:].rearrange("p (c l) -> p c l", l=64)[:, :, 63]
                S.op("dve", lambda e: e.tensor_tensor_scan(out=mseq[:], data0=blast, data1=alast,
                                                           initial=mcar[:, 0:1], op0=ALU.add, op1=ALU.max),
                     reads=[R["bcum"], R["g"], mcar], writes=[mseq])
                S.op("dve", lambda e: e.tensor_copy(out=mprev[:, 0:1], in_=mcar[:, 0:1]), reads=[mcar], writes=[mprev])
                S.op("dve", lambda e: e.tensor_copy(out=mprev[:, 1:8], in_=mseq[:, 0:7]), reads=[mseq], writes=[mprev])
                S.op("dve", lambda e: e.tensor_copy(out=mcar[:, 0:1], in_=mseq[:, 7:8]), reads=[mseq], writes=[mcar])
                v3 = lambda t_: t_[:].rearrange("p (c l) -> p c l", l=64)
                bc8 = lambda t_: t_[:, :].unsqueeze(2).broadcast_to([4, 8, 64])
                S.op("dve", lambda e: e.tensor_tensor(out=v3(R["bm"]), in0=v3(R["bcum"]), in1=bc8(mprev), op=ALU.add),
                     reads=[R["bcum"], mprev], writes=[R["bm"]])
                S.op("dve", lambda e: e.tensor_tensor(out=R["mt"][:], in0=R["bm"][:], in1=R["g"][:], op=ALU.max),
                     reads=[R["bm"], R["g"]], writes=[R["mt"]])
                S.op("dve", lambda e: e.tensor_tensor(out=R["bm"][:], in0=R["bm"][:], in1=R["mt"][:],
                                                      op=ALU.subtract), reads=[R["bm"], R["mt"]], writes=[R["bm"]])
                S.op("act", lambda e: e.activation(out=R["dint"][:], in_=R["bm"][:], func=AF.Exp),
                     reads=[R["bm"]], writes=[R["dint"]])
                S.op("dve", lambda e: e.tensor_tensor(out=R["r1"][:], in0=R["bcum"][:], in1=R["mt"][:],
                                                      op=ALU.subtract), reads=[R["bcum"], R["mt"]], writes=[R["r1"]])
                S.op("act", lambda e: e.activation(out=R["emt"][:], in_=R["mt"][:], func=AF.Exp, scale=-1.0),
                     reads=[R["mt"]], writes=[R["emt"]])
                S.op("dve", lambda e: e.tensor_tensor(out=dlt[:], in0=blast, in1=alast, op=ALU.subtract),
                     reads=[R["bcum"], R["g"]], writes=[dlt])
                S.op("dve", lambda e: e.tensor_tensor(out=v3(R["bm"]), in0=v3(R["a"]), in1=bc8(dlt), op=ALU.add),
                     reads=[R["a"], dlt], writes=[R["bm"]])
                S.op("act", lambda e: e.activation(out=R["wend"][:], in_=R["bm"][:], func=AF.Exp),
                     reads=[R["bm"]], writes=[R["wend"]])
                S.op("dve", lambda e: e.tensor_tensor(out=dold_loc[:, 0:8], in0=blast, in1=mprev[:], op=ALU.add),
                     reads=[R["bcum"], mprev], writes=[dold_loc])
                S.op("dve", lambda e: e.tensor_tensor(out=dold_loc[:, 0:8], in0=dold_loc[:, 0:8], in1=mseq[:],
                                                      op=ALU.subtract), reads=[dold_loc, mseq], writes=[dold_loc])
                S.op("dve", lambda e: e.tensor_tensor(out=dold_loc[:, 8:16], in0=alast, in1=mseq[:], op=ALU.subtract),
                     reads=[R["g"], mseq], writes=[dold_loc])
                S.op("act", lambda e: e.activation(out=dold_loc[:], in_=dold_loc[:], func=AF.Exp),
                     reads=[dold_loc], writes=[dold_loc])
                ps = nbank()
                for i in range(4):
                    for qi, nm in enumerate(("emt", "wend")):
                        o_ = (i * 2 + qi) * 4
                        S.op("pe", lambda e: e.matmul(ps[:, o_:o_ + 4], lhsT=R[nm][:, i * 128:(i + 1) * 128], rhs=i4v,
                                                      start=True, stop=True, skip_group_check=True),
                             reads=[R[nm], cf], writes=[ps])
                S.op("dve", lambda e: e.tensor_copy(out=colsb[:].rearrange("p a b c -> p (a b c)"), in_=ps[:, 0:32]),
                     reads=[ps], writes=[colsb])
                ps = nbank()
                for h in range(NH):
                    S.op("pe", lambda e: e.matmul(ps[:, h * 16:(h + 1) * 16], lhsT=selv(h), rhs=dold_loc[:],
                                                  start=True, stop=True, skip_group_check=True),
                         reads=[cf, dold_loc], writes=[ps])
                S.op("dve", lambda e: e.tensor_copy(out=dcol[:].rearrange("p h c -> p (h c)"), in_=ps[:, 0:64]),
                     reads=[ps], writes=[dcol])
                for h in range(NH):
                    ps = nbank()
                    S.op("pe", lambda e: e.matmul(ps[:, :], lhsT=selv(h), rhs=R["dint"][:], start=True, stop=True),
                         reads=[cf, R["dint"]], writes=[ps])
                    S.op("dve", lambda e: e.tensor_tensor(out=qdT[:, h, :], in0=ps[:, :], in1=qT[:, h, :],
                                                          op=ALU.mult), reads=[ps, qT], writes=[qdT])
                for i in range(4):
                    ps = nbank()
                    psb = ps[:].bitcast(BF16)
                    for h in range(NH):
                        S.op("pe", lambda e: e.transpose(out=psb[:, h * 128:(h + 1) * 128],
                                                         in_=kT[:, h, i * 128:(i + 1) * 128], identity=identb[:]),
                             reads=[kT, identb], writes=[ps])
                    S.op("dve", lambda e: e.tensor_tensor(
                        out=kw[:, i, :, :], in0=psb[:, 0:512].rearrange("p (h d) -> p h d", d=128),
                        in1=colsb[:, i, 1, :].unsqueeze(2).broadcast_to([128, NH, 128]), op=ALU.mult),
                         reads=[ps, colsb], writes=[kw])
                for c in range(8):
                    i, half_ = c // 2, (c % 2) * 64
                    for h in range(NH):
                        S.op("act", lambda e: e.copy(out=Cb[:, h, c, :], in_=Cst[:, h, :]), reads=[Cst], writes=[Cb])
                        ps = nbank()
                        S.op("pe", lambda e: e.matmul(ps[:, 0:129], lhsT=kw[half_:half_ + 64, i, h, :],
                                                      rhs=vaug[half_:half_ + 64, i, h, :], start=True, stop=True),
                             reads=[kw, vaug], writes=[ps])
                        S.op("dve", lambda e: e.tensor_scalar(out=Ctmp[:], in0=Cst[:, h, :],
                                                              scalar1=dcol[:, h, c:c + 1], scalar2=None,
                                                              op0=ALU.mult), reads=[Cst, dcol], writes=[Ctmp])
                        S.op("dve", lambda e: e.scalar_tensor_tensor(out=Cst[:, h, :], in0=ps[:, 0:129],
                                                                     scalar=dcol[:, h, 8 + c:9 + c], in1=Ctmp[:],
                                                                     op0=ALU.mult, op1=ALU.add),
                             reads=[ps, dcol, Ctmp], writes=[Cst])
                for i in range(4):
                    pA = [banks[4], banks[5]]
                    for h in range(NH):
                        ps = banks[0 + (h % 2)]
                        S.op("pe", lambda e: e.matmul(ps[:, 0:128], lhsT=kT[:, h, i * 128:(i + 1) * 128],
                                                      rhs=qT[:, h, i * 128:(i + 1) * 128], start=True, stop=True),
                             reads=[kT, qT], writes=[ps])
                        pd_ = banks[2 + (h % 2)]
                        S.op("pe", lambda e: e.matmul(pd_[:, 0:128], lhsT=selv(h),
                                                      rhs=R["r1"][:, i * 128:(i + 1) * 128],
                                                      start=True, stop=False), reads=[cf, R["r1"]], writes=[pd_])
                        S.op("pe", lambda e: e.matmul(pd_[:, 0:128], lhsT=R["a"][:, i * 128:(i + 1) * 128],
                                                      rhs=selv(h), start=False, stop=False),
                             reads=[cf, R["a"]], writes=[pd_])
                        S.op("pe", lambda e: e.matmul(pd_[:, 0:128], lhsT=identb[:], rhs=mlmaskb[:], start=False,
                                                      stop=True), reads=[identb, mlmaskb], writes=[pd_])
                        dm = Dm[h % 2]
                        S.op("act", lambda e: e.activation(out=dm[:], in_=pd_[:, 0:128], func=AF.Exp),
                             reads=[pd_], writes=[dm])
                        sc = scT[h % 2]
                        S.op("dve", lambda e: e.tensor_tensor(out=sc[:], in0=ps[:, 0:128], in1=dm[:], op=ALU.mult),
                             reads=[ps, dm], writes=[sc])
                        po = pA[h // 2]
                        oo = (h % 2) * 129
                        S.op("pe", lambda e: e.matmul(po[:, oo:oo + 129], lhsT=sc[:], rhs=vaug[:, i, h, :],
                                                      start=True, stop=False, skip_group_check=True),
                             reads=[sc, vaug], writes=[po])
                        for hf in range(2):
                            c = 2 * i + hf
                            S.op("pe", lambda e: e.matmul(po[hf * 64:(hf + 1) * 64, oo:oo + 129],
                                                          lhsT=qdT[:, h, i * 128 + hf * 64:i * 128 + (hf + 1) * 64],
                                                          rhs=Cb[:, h, c, :], start=False, stop=(hf == 1),
                                                          skip_group_check=True),
                                 reads=[qdT, Cb], writes=[po])
                    for hp in range(2):
                        po = pA[hp]
                        pv = po[:, 0:258].rearrange("p (h e) -> p h e", e=129)
                        S.op("dve", lambda e: e.tensor_copy(out=den4[:, 2 * hp:2 * hp + 2], in_=pv[:, :, 128]),
                             reads=[po], writes=[den4])
                    S.op("dve", lambda e: e.tensor_scalar(out=st1[:], in0=den4[:], scalar1=-1.0, scalar2=None,
                                                          op0=ALU.mult), reads=[den4], writes=[st1])
                    S.op("dve", lambda e: e.tensor_tensor(out=den4[:], in0=den4[:], in1=st1[:], op=ALU.max),
                         reads=[den4, st1], writes=[den4])
                    S.op("dve", lambda e: e.tensor_tensor(out=den4[:], in0=den4[:], in1=colsb[:, i, 0, :],
                                                          op=ALU.max), reads=[den4, colsb], writes=[den4])
                    S.op("dve", lambda e: e.reciprocal(out=den4[:], in_=den4[:]), reads=[den4], writes=[den4])
                    for hp in range(2):
                        po = pA[hp]
                        pv = po[:, 0:258].rearrange("p (h e) -> p h e", e=129)
                        S.op("dve", lambda e: e.tensor_tensor(
                            out=hm[:, 2 * hp:2 * hp + 2, :], in0=pv[:, :, 0:128],
                            in1=den4[:, 2 * hp:2 * hp + 2].unsqueeze(2).broadcast_to([128, 2, 128]), op=ALU.mult),
                             reads=[po, den4], writes=[hm])
                    head_norm_tok(hm, hsq)
                    S.op("dve", lambda e: e.tensor_tensor(out=ytokb[:], in0=hm[:].rearrange("p h d -> p (h d)"),
                                                          in1=oz[:, i, :], op=ALU.mult), reads=[hm, oz],
                         writes=[ytokb])
                    tok_to_fm(ytokb, ysT[1], i)

            with branch():
                set_gen(range(8))
                sig = [sb("sig%d" % i, [128, TG]) for i in range(3)]
                mrg = sb("mrg", [128, 8, TG], BF16)
                mrgacc = sb("mrgacc", [128, 8, TG])
                macc = [sb("macc%d" % i, [128, TG]) for i in range(3)]
                otk = [sb("otk%d" % i, [128, D]) for i in range(4)]
                xr = [sb("xr%d" % i, [128, D]) for i in range(2)]
                junk = sb("junk", [128, D])
                uc = 0
                for n in range(5):
                    for half in range(2):
                        wg = loadW(wb[l], O_GATES + n * 1024 + half * 512, 512)
                        t = Wt[wt_rr[0] % len(Wt)]
                        wt_rr[0] += 1
                        S.dma(t[:, 0:4, 0:512],
                              wupb[l][n * W:(n + 1) * W, half * 512:(half + 1) * 512].rearrange("(c p) n -> p c n",
                                                                                                  p=128),
                              reads=[d_wupb], writes=[t])
                        for dq in range(4):
                            dc = half * 4 + dq
                            sg = sig[uc % 3]
                            ma = macc[uc % 3]
                            uc += 1
                            ps = nbank()
                            proj_fm(wg, dq, hT, ps, ps[:, :])
                            S.op("act", lambda e: e.activation(out=sg[:], in_=ps[:, :], func=AF.Sigmoid),
                                 reads=[ps], writes=[sg])
                            ps2 = nbank()
                            mm_acc(ps2, ps2[:, :], [(t[:, c, dq * 128:(dq + 1) * 128], ysT[n][:, c, :])
                                                    for c in range(4)], extra_reads=[t, ysT[n]])
                            if n == 0:
                                S.op("dve", lambda e: e.tensor_tensor(out=mrgacc[:, dc, :], in0=ps2[:, :], in1=sg[:],
                                                                      op=ALU.mult), reads=[ps2, sg], writes=[mrgacc])
                            else:
                                S.op("dve", lambda e: e.tensor_tensor(out=ma[:], in0=ps2[:, :], in1=sg[:],
                                                                      op=ALU.mult), reads=[ps2, sg], writes=[ma])
                                if n < 4:
                                    S.op("pool", lambda e: e.tensor_tensor(out=mrgacc[:, dc, :], in0=mrgacc[:, dc, :],
                                                                           in1=ma[:], op=ALU.add),
                                         reads=[mrgacc, ma], writes=[mrgacc])
                                else:
                                    S.op("pool", lambda e: e.tensor_tensor(out=mrg[:, dc, :], in0=mrgacc[:, dc, :],
                                                                           in1=ma[:], op=ALU.add),
                                         reads=[mrgacc, ma], writes=[mrg])
                for hc in range(2):
                    wt = loadW(woutb[l], hc * 512, 512, d_woutb)
                    for i in range(4):
                        ps = nbank()
                        proj_tm(wt, i, mrg, ps, ps[:, :])
                        S.op("act", lambda e: e.copy(out=otk[i][:, hc * 512:(hc + 1) * 512], in_=ps[:, :]),
                             reads=[ps], writes=[otk[i]])
                for i in range(4):
                    S.op("act", lambda e: e.activation(out=junk[:], in_=otk[i][:], func=AF.Square,
                                                       accum_out=ss4[:, i:i + 1]),
                         reads=[otk[i]], writes=[junk, ss4])
                rsqrt_cols(rstd4, ss4, 4, 1.0 / D)
                for i in range(4):
                    x_ = xr[i % 2]
                    S.dma(x_[:], xsrc[t0 + i * 128:t0 + (i + 1) * 128, :],
                          reads=[xsrc_t] if xsrc_t is not None else [], writes=[x_])
                    S.op("dve", lambda e: e.scalar_tensor_tensor(out=otk[i][:], in0=otk[i][:],
                                                                 scalar=rstd4[:, i:i + 1], in1=gpost[:],
                                                                 op0=ALU.mult, op1=ALU.mult),
                         reads=[otk[i], rstd4, gpost], writes=[otk[i]])
                    S.op("pool", lambda e: e.tensor_tensor(out=otk[i][:], in0=otk[i][:], in1=x_[:], op=ALU.add),
                         reads=[otk[i], x_], writes=[otk[i]])
                    S.dma(xdst[t0 + i * 128:t0 + (i + 1) * 128, :], otk[i][:], reads=[otk[i]], writes=[xdst_t])
    S.finish()
    print("instructions:", S.ninst, "sems:", S.nsem, "sbuf max bytes/partition:", max_bytes[0])
    return nc, carr, amask_np


_CACHE = {}


def kernel(**inputs):
    TT = inputs["x"].shape[1]
    if TT not in _CACHE:
        _CACHE[TT] = build(TT)
    nc, carr, amask_np = _CACHE[TT]
    m = {}
    for k, v in inputs.items():
        a = np.asarray(v)
        if k in ("x", "mem"):
            a = a[0]
        m[k] = np.ascontiguousarray(a)
    m["cst"] = carr
    m["cst_am"] = amask_np
    res = run_bass_kernel_spmd(nc, [dict(m) for _ in range(8)], core_ids=list(range(8)))
    y = np.asarray(res.results[0]["y"], dtype=np.float32)
    return y[None, :, :]
```
